# Optimizing a Trainium2 kernel written in Bass

```python
import jax, jax.numpy as jnp
from jax import lax
import numpy as np

D_MODEL = 1024
BATCH = 8
SEQ = 2048
DEPTH = 4

POOL_GROUPS = 4
POOL_GROUP_DIM = 128
POOL_WIDTH = POOL_GROUPS * POOL_GROUP_DIM
POOL_WINDOWS = (2, 4, 8, 16)
LRU_HEADS = 10
LRU_HEAD_DIM = 128
LRU_WIDTH = LRU_HEADS * LRU_HEAD_DIM
CONV_WIDTH = 4
LRU_C = 8.0
D_FF = 2816
EPS = 1e-6
IN_WIDTH = POOL_WIDTH + 2 * LRU_WIDTH + 2 * D_MODEL

kernel_name = "macaron_pool_rglru_gated_hybrid"


def rmsnorm(x, g):
    xf = x.astype(jnp.float32)
    var = jnp.mean(xf * xf, axis=-1, keepdims=True)
    return (xf * lax.rsqrt(var + EPS) * g.astype(jnp.float32)).astype(x.dtype)


def swiglu_ffn(h, w_up, w_down):
    u = h @ w_up
    a, b = jnp.split(u, 2, axis=-1)
    return (jax.nn.silu(a) * b) @ w_down


def causal_pool_minus_self(u, window):
    b, s, c = u.shape
    uf = u.astype(jnp.float32)
    cs = jnp.cumsum(uf, axis=1)
    cs_pad = jnp.concatenate([jnp.zeros((b, 1, c), jnp.float32), cs], axis=1)
    prev = jnp.concatenate([jnp.zeros((b, window - 1, c), jnp.float32), cs_pad[:, : s - window + 1]], axis=1)
    count = jnp.minimum(jnp.arange(1, s + 1, dtype=jnp.float32), float(window))[None, :, None]
    return ((cs - prev) / count - uf).astype(u.dtype)


def pool_mixer(u, w_grp, b_grp, scale):
    b, s, _ = u.shape
    ug = u.reshape(b, s, POOL_GROUPS, POOL_GROUP_DIM)
    pooled = jnp.stack([causal_pool_minus_self(ug[:, :, g], POOL_WINDOWS[g]) for g in range(POOL_GROUPS)], axis=2)
    mixed = jnp.einsum('bsgc,gcd->bsgd', pooled, w_grp) + b_grp
    return mixed.reshape(b, s, POOL_WIDTH) * scale


def causal_depthwise_conv(u, w, bias):
    s = u.shape[1]
    up = jnp.pad(u, ((0, 0), (CONV_WIDTH - 1, 0), (0, 0)))
    y = sum(up[:, k:k + s] * w[k] for k in range(CONV_WIDTH))
    return y + bias


def _lru_combine(left, right):
    a_l, b_l = left
    a_r, b_r = right
    return a_l * a_r, a_r * b_l + b_r


def rg_lru(u, w_a, b_a, w_x, b_x, lam):
    b, s, _ = u.shape
    uh = u.reshape(b, s, LRU_HEADS, LRU_HEAD_DIM)
    r = jax.nn.sigmoid(jnp.einsum('bshd,hde->bshe', uh, w_a) + b_a).reshape(b, s, LRU_WIDTH)
    i = jax.nn.sigmoid(jnp.einsum('bshd,hde->bshe', uh, w_x) + b_x).reshape(b, s, LRU_WIDTH)
    log_a = -LRU_C * r.astype(jnp.float32) * jax.nn.softplus(-lam.astype(jnp.float32))
    a = jnp.exp(log_a)
    mult = jnp.sqrt(-jnp.expm1(2.0 * log_a))
    bx = mult * (i * u).astype(jnp.float32)
    _, h = lax.associative_scan(_lru_combine, (a, bx), axis=1)
    return h.astype(u.dtype)


def hybrid_mixer(h, w_in, pool_w, pool_b, pool_scale, w_pool_up, conv_w, conv_b,
                 lru_w_a, lru_b_a, lru_w_x, lru_b_x, lru_lambda, w_lru_up, w_out):
    proj = h @ w_in
    s1 = POOL_WIDTH
    s2 = s1 + LRU_WIDTH
    s3 = s2 + LRU_WIDTH
    u_pool, u_lru, u_gelu, g_logits = proj[..., :s1], proj[..., s1:s2], proj[..., s2:s3], proj[..., s3:]
    y_pool = pool_mixer(u_pool, pool_w, pool_b, pool_scale) @ w_pool_up
    v = causal_depthwise_conv(u_lru, conv_w, conv_b)
    y_lru = (rg_lru(v, lru_w_a, lru_b_a, lru_w_x, lru_b_x, lru_lambda) * jax.nn.gelu(u_gelu)) @ w_lru_up
    g = jax.nn.sigmoid(g_logits)
    g_pool, g_lru = g[..., :D_MODEL], g[..., D_MODEL:]
    return (g_pool * y_pool + g_lru * y_lru) @ w_out


def setup_inputs(seed: int = 0) -> dict:
    key = jax.random.key(seed)
    ks = jax.random.split(key, 26)
    f32 = jnp.float32

    def nrm(k, shape, fan_in):
        return jax.random.normal(k, shape, f32) * (fan_in ** -0.5)

    def gain(k, shape):
        return 1.0 + 0.02 * jax.random.normal(k, shape, f32)

    def small(k, shape):
        return 0.01 * jax.random.normal(k, shape, f32)

    L = DEPTH
    a_c = jax.random.uniform(ks[17], (L, LRU_WIDTH), f32, 0.9, 0.999)
    sig = a_c ** (1.0 / LRU_C)
    lam = jnp.log(sig) - jnp.log1p(-sig)
    return {
        "x": jax.random.normal(ks[0], (BATCH, SEQ, D_MODEL), f32),
        "norm_ffn1": gain(ks[1], (L, D_MODEL)),
        "ffn1_w_up": nrm(ks[2], (L, D_MODEL, 2 * D_FF), D_MODEL),
        "ffn1_w_down": nrm(ks[3], (L, D_FF, D_MODEL), D_FF),
        "norm_mix": gain(ks[4], (L, D_MODEL)),
        "w_in": nrm(ks[5], (L, D_MODEL, IN_WIDTH), D_MODEL),
        "pool_w": nrm(ks[6], (L, POOL_GROUPS, POOL_GROUP_DIM, POOL_GROUP_DIM), POOL_GROUP_DIM),
        "pool_b": small(ks[7], (L, POOL_GROUPS, POOL_GROUP_DIM)),
        "pool_scale": 1.0 + 0.1 * jax.random.normal(ks[8], (L, POOL_WIDTH), f32),
        "w_pool_up": nrm(ks[9], (L, POOL_WIDTH, D_MODEL), POOL_WIDTH),
        "conv_w": nrm(ks[10], (L, CONV_WIDTH, LRU_WIDTH), CONV_WIDTH),
        "conv_b": small(ks[11], (L, LRU_WIDTH)),
        "lru_w_a": nrm(ks[12], (L, LRU_HEADS, LRU_HEAD_DIM, LRU_HEAD_DIM), LRU_HEAD_DIM),
        "lru_b_a": small(ks[13], (L, LRU_HEADS, LRU_HEAD_DIM)),
        "lru_w_x": nrm(ks[14], (L, LRU_HEADS, LRU_HEAD_DIM, LRU_HEAD_DIM), LRU_HEAD_DIM),
        "lru_b_x": small(ks[15], (L, LRU_HEADS, LRU_HEAD_DIM)),
        "lru_lambda": lam,
        "w_lru_up": nrm(ks[16], (L, LRU_WIDTH, D_MODEL), LRU_WIDTH),
        "w_out": nrm(ks[18], (L, D_MODEL, D_MODEL), D_MODEL),
        "norm_ffn2": gain(ks[19], (L, D_MODEL)),
        "ffn2_w_up": nrm(ks[20], (L, D_MODEL, 2 * D_FF), D_MODEL),
        "ffn2_w_down": nrm(ks[21], (L, D_FF, D_MODEL), D_FF),
        "final_norm": gain(ks[22], (D_MODEL,)),
    }


def reference(x, norm_ffn1, ffn1_w_up, ffn1_w_down, norm_mix, w_in, pool_w, pool_b, pool_scale,
              w_pool_up, conv_w, conv_b, lru_w_a, lru_b_a, lru_w_x, lru_b_x, lru_lambda, w_lru_up,
              w_out, norm_ffn2, ffn2_w_up, ffn2_w_down, final_norm):
    for l in range(DEPTH):
        x = x + 0.5 * swiglu_ffn(rmsnorm(x, norm_ffn1[l]), ffn1_w_up[l], ffn1_w_down[l])
        x = x + hybrid_mixer(rmsnorm(x, norm_mix[l]), w_in[l], pool_w[l], pool_b[l], pool_scale[l],
                             w_pool_up[l], conv_w[l], conv_b[l], lru_w_a[l], lru_b_a[l],
                             lru_w_x[l], lru_b_x[l], lru_lambda[l], w_lru_up[l], w_out[l])
        x = x + 0.5 * swiglu_ffn(rmsnorm(x, norm_ffn2[l]), ffn2_w_up[l], ffn2_w_down[l])
    return rmsnorm(x, final_norm)
```

```python
import numpy as np
from contextlib import ExitStack
import concourse.bass as bass
import concourse.mybir as mybir
from concourse.bass_utils import run_bass_kernel_spmd

F32 = mybir.dt.float32
BF16 = mybir.dt.bfloat16
AF = mybir.ActivationFunctionType
ALU = mybir.AluOpType

D = 1024
S_LEN = 2048
NT = 4
TS = 512
NC8 = 8
DFF = 2816
NJ = 22
NH = 10
NG = 4
EPS = 1e-6
NSLOT = 5
SLOT_COLS = 3072
TW = 528
NTMP = 14
VPL = 112


class Sem:
    def __init__(self, h):
        self.h = h
        self.total = 0


class Ins:
    __slots__ = ("eng", "emit", "deps", "sig", "sem", "inc", "done", "isdma", "idx", "dur", "tset", "alld",
                 "succ", "nleft", "rt", "fin", "tag", "lab")

    def __init__(self, eng, emit, sem, inc, isdma):
        self.eng = eng
        self.emit = emit
        self.deps = []
        self.sig = isdma
        self.sem = sem
        self.inc = inc
        self.done = None
        self.isdma = isdma
        self.idx = 0
        self.dur = 500.0
        self.tset = None
        self.alld = []
        self.succ = []
        self.nleft = 0
        self.rt = 0.0
        self.fin = 0.0


class Sched:
    ENGS = ("pe", "act", "dve", "pool", "sp")

    def __init__(self):
        self.dry = True
        self.reset()

    def reset(self):
        self.all = []
        self.q = {e: [] for e in self.ENGS}
        self.recs = {}
        self.esem = {}

    def add(self, eng, emit, reads=(), writes=(), sem=None, inc=1, isdma=False, dur=500.0, tset=None, lab=""):
        if self.dry:
            return None
        if sem is None:
            sem = self.esem[eng]
        I = Ins(eng, emit, sem, inc, isdma)
        I.idx = len(self.all)
        I.tag = getattr(self, "tag", "")
        I.lab = lab
        I.dur = dur
        I.tset = tset
        deps = {}
        recs = self.recs
        for (tid, lo, hi) in reads:
            for r in recs.get(tid, ()):
                if r[0] < hi and lo < r[1] and r[2] is not None:
                    deps[id(r[2])] = (r[2], "RAW")
        for (tid, lo, hi) in writes:
            for r in recs.get(tid, ()):
                if r[0] < hi and lo < r[1]:
                    if r[2] is not None and id(r[2]) not in deps:
                        deps[id(r[2])] = (r[2], "WAW")
                    for rd in r[3]:
                        if id(rd) not in deps:
                            deps[id(rd)] = (rd, "WAR")
        for (tid, lo, hi) in reads:
            lst = recs.setdefault(tid, [])
            for r in lst:
                if r[0] == lo and r[1] == hi:
                    r[3].append(I)
                    break
            else:
                lst.append([lo, hi, None, [I]])
        for (tid, lo, hi) in writes:
            lst = recs.setdefault(tid, [])
            for r in lst:
                if r[0] == lo and r[1] == hi:
                    r[2] = I
                    r[3] = []
                    break
            else:
                lst.append([lo, hi, I, []])
        for (Dp, kind) in deps.values():
            if Dp is I:
                continue
            I.alld.append(Dp)
            if Dp.eng == eng and not Dp.isdma:
                if eng == "pe":
                    continue
                if kind != "RAW":
                    continue
            I.deps.append(Dp)
            Dp.sig = True
        self.all.append(I)
        self.q[eng].append(I)
        return I

    def schedule(self):
        HOP = 900.0
        SWITCH = 2700.0
        PATIENCE = 2500.0
        for I in self.all:
            I.nleft = len(I.alld)
            I.rt = 0.0
            for Dp in I.alld:
                Dp.succ.append(I)
        ready = {e: [] for e in self.ENGS}
        for I in self.all:
            if I.nleft == 0:
                ready[I.eng].append(I)
        free = {e: 0.0 for e in self.ENGS}
        order = {e: [] for e in self.ENGS}
        cur_set = None
        left = len(self.all)
        while left:
            best_e, best_t = None, None
            for e in self.ENGS:
                r = ready[e]
                if not r:
                    continue
                t = max(free[e], min(x.rt for x in r))
                if best_t is None or t < best_t:
                    best_e, best_t = e, t
            e, t = best_e, best_t
            r = ready[e]
            cands = [x for x in r if x.rt <= t]
            I = None
            if e == "act":
                oldest = min(x.idx for x in cands)
                same = [x for x in r if (x.tset is None or cur_set in x.tset) and x.rt <= t + PATIENCE]
                if same:
                    so = min(same, key=lambda x: (max(x.rt, t), x.idx))
                    if so.idx - oldest < 600:
                        I = so
                        t = max(t, so.rt)
            if I is None:
                I = min(cands, key=lambda x: x.idx)
            r.remove(I)
            start = t
            if e == "act" and I.tset is not None and cur_set not in I.tset:
                start += SWITCH
                cur_set = sorted(I.tset)[0]
            if I.isdma:
                free[e] = start + 900.0
                I.fin = start + I.dur
            else:
                I.fin = start + I.dur
                free[e] = I.fin
            order[e].append(I)
            left -= 1
            for sc in I.succ:
                lat = I.fin + (HOP if (sc.eng != e or I.isdma) else 0.0)
                if lat > sc.rt:
                    sc.rt = lat
                sc.nleft -= 1
                if sc.nleft == 0:
                    ready[sc.eng].append(sc)
        self.q = order
        self.sim_end = max(free.values())

    def finalize(self, whole_sems=(), reorder=True):
        if reorder:
            self.schedule()
        for e in self.ENGS:
            for I in self.q[e]:
                if I.sig:
                    I.sem.total += I.inc
                    I.done = I.sem.total
        for I in self.all:
            if I.sig and any(I.sem is w for w in whole_sems):
                I.done = I.sem.total

    def emit_engine(self, engname, eng, tail=None):
        waited = {}
        for I in self.q[engname]:
            need = {}
            for Dp in I.deps:
                k = id(Dp.sem)
                if need.get(k, (None, 0))[1] < Dp.done:
                    need[k] = (Dp.sem, Dp.done)
            for k, (sem, val) in need.items():
                if waited.get(k, 0) < val:
                    eng.wait_ge(sem.h, val)
                    waited[k] = val
            r = I.emit(eng)
            if I.sig:
                if isinstance(r, list):
                    for x in r:
                        x.then_inc(I.sem.h, 16)
                else:
                    r.then_inc(I.sem.h, I.inc)
        if tail is not None:
            tail(eng)


class TT:
    def __init__(self, name, h):
        self.name = name
        self.h = h

    def v(self, lo, hi, rlo=None, rhi=None):
        return (self.h[:, lo:hi], (self.name, lo if rlo is None else rlo, hi if rhi is None else rhi))


class WStream:
    def __init__(self, S):
        self.S = S
        self.uses = []
        self.order = []
        self.last_pos = {}
        self.src_of = {}
        self.reset_run()

    def reset_run(self):
        self.pos = 0
        self.nloaded = 0
        self.slot_of = {}

    def finish_dry(self):
        seen = set()
        for i, k in enumerate(self.uses):
            if k not in seen:
                seen.add(k)
                self.order.append(k)
            self.last_pos[k] = i

    def get(self, key, src, n):
        if self.S.dry:
            self.uses.append(key)
            self.src_of[key] = (src, n)
            return 0
        assert self.uses[self.pos] == key
        self.pos += 1
        assert key in self.slot_of, ("slab not resident", key)
        idx = self.order.index(key) if False else self.slot_of[key]
        return idx

    def tick(self, loader):
        if self.S.dry:
            return
        while self.nloaded < len(self.order):
            n = self.nloaded
            if n >= NSLOT:
                old = self.order[n - NSLOT]
                if self.last_pos[old] >= self.pos:
                    break
                self.slot_of.pop(old, None)
            key = self.order[n]
            slot = n % NSLOT
            loader(key, slot)
            self.slot_of[key] = slot
            self.nloaded += 1


def _rows_slab(W, r0, nk, c0, ncols):
    blk = W[r0:r0 + nk * 128, c0:c0 + ncols].reshape(nk, 128, ncols)
    return np.ascontiguousarray(blk.transpose(1, 0, 2)).reshape(128, nk * ncols)


def slab_source(src, inp):
    kind = src[0]
    l = src[1]
    if kind == "up":
        W = inp["ffn1_w_up" if src[2] == 0 else "ffn2_w_up"][l]
        return _rows_slab(W, 0, 8, src[3], src[4])
    if kind == "wd":
        W = inp["ffn1_w_down" if src[2] == 0 else "ffn2_w_down"][l]
        return _rows_slab(W, src[3] * 11 * 128, 11, src[4], src[5])
    if kind == "win":
        return _rows_slab(inp["w_in"][l], 0, 8, src[2], src[3])
    if kind == "small":
        pw = inp["pool_w"][l]
        wa = inp["lru_w_a"][l]
        wx = inp["lru_w_x"][l]
        allw = np.concatenate([pw, wa, wx], axis=0)
        return np.ascontiguousarray(allw.transpose(1, 0, 2)).reshape(128, 24 * 128)
    if kind == "head":
        h = src[2]
        W = inp["w_in"][l]
        blks = [W[k * 128:(k + 1) * 128, 512 + 128 * h:512 + 128 * (h + 1)] for k in range(8)]
        blks += [W[k * 128:(k + 1) * 128, 1792 + 128 * h:1792 + 128 * (h + 1)] for k in range(8)]
        blks += [inp["lru_w_a"][l][h], inp["lru_w_x"][l][h]]
        return np.ascontiguousarray(np.stack(blks, axis=1)).reshape(128, 18 * 128)
    if kind == "pool1":
        W = inp["w_in"][l]
        blks = [W[k * 128:(k + 1) * 128, 384:512] for k in range(8)] + [inp["pool_w"][l][g] for g in range(4)]
        return np.ascontiguousarray(np.stack(blks, axis=1)).reshape(128, 12 * 128)
    if kind == "m3":
        c = src[2]
        W = inp["w_in"][l]
        blks = [W[k * 128:(k + 1) * 128, 3072 + 128 * c:3072 + 128 * (c + 1)] for k in range(8)]
        blks += [W[k * 128:(k + 1) * 128, 4096 + 128 * c:4096 + 128 * (c + 1)] for k in range(8)]
        PU = inp["w_pool_up"][l]
        blks += [PU[g * 128:(g + 1) * 128, 128 * c:128 * (c + 1)] for g in range(4)]
        return np.ascontiguousarray(np.stack(blks, axis=1)).reshape(128, 20 * 128)
    if kind == "pu":
        return _rows_slab(inp["w_pool_up"][l], 0, 4, src[2], src[3])
    if kind == "lu":
        return _rows_slab(inp["w_lru_up"][l], 0, 10, src[2], src[3])
    if kind == "wo":
        return _rows_slab(inp["w_out"][l], 0, 8, src[2], src[3])
    raise KeyError(src)


def pack_vecs(inp, nl):
    cols = []

    def fm(v, nchunk):
        return np.asarray(v, np.float32).reshape(nchunk, 128).T

    for l in range(nl):
        cols.append(fm(inp["norm_ffn1"][l], 8))
        cols.append(fm(inp["norm_mix"][l], 8))
        cols.append(fm(inp["norm_ffn2"][l], 8))
        cols.append(fm(inp["pool_b"][l].reshape(-1), 4))
        cols.append(fm(inp["pool_scale"][l], 4))
        cw = np.asarray(inp["conv_w"][l], np.float32)
        cwf = cw.reshape(4, NH, 128).transpose(2, 1, 0).reshape(128, NH * 4)
        cols.append(cwf)
        cols.append(fm(inp["conv_b"][l], NH))
        cols.append(fm(inp["lru_b_a"][l].reshape(-1), NH))
        cols.append(fm(inp["lru_b_x"][l].reshape(-1), NH))
        cols.append(fm(inp["lru_lambda"][l], NH))
    cols.append(fm(inp["final_norm"], 8))
    return np.ascontiguousarray(np.concatenate(cols, axis=1), dtype=np.float32)


class Gen:
    def __init__(self, nl):
        self.nl = nl
        self.S = Sched()
        self.W = WStream(self.S)
        self.pbi = 0

    def pb(self):
        b = self.P[self.pbi % 8]
        self.pbi += 1
        return b.v(0, TS)

    def xt(self, c, t):
        return self.XT.v(c * S_LEN + t * TS, c * S_LEN + (t + 1) * TS)

    def xn(self, c, t):
        return self.XN.v(c * S_LEN + t * TS, c * S_LEN + (t + 1) * TS)

    def tmp(self, i, lo=0, hi=TS):
        return self.TMP.v(i * TW + lo, i * TW + hi, i * TW, (i + 1) * TW)

    def scrf(self, j):
        lo = 14336 + j * 1024
        ap = self.SCR.h[:, lo:lo + 1024]
        return (ap.bitcast(F32) if ap is not None else None, ("SCR", lo, lo + 1024))

    def sqb(self, i):
        return self.SQB.v(i * TS, (i + 1) * TS)

    def vec(self, col, n=1):
        return self.VEC.v(col, col + n, 0, self.nvec)

    def der(self, col, n=1):
        return self.DER.v(col, col + n)

    @staticmethod
    def _sb(x, reads):
        if isinstance(x, tuple):
            reads.append(x[1])
            return x[0]
        return x

    def act(self, out, in_, func, scale=1.0, bias=0.0):
        reads = [in_[1]]
        sc = self._sb(scale, reads)
        bi = self._sb(bias, reads)
        o, i = out[0], in_[0]
        n = 0 if o is None else o.shape[-1]
        ts_ = {AF.Silu: frozenset("S"), AF.Gelu_apprx_tanh: frozenset("G"), AF.Tanh: frozenset("GS"),
               AF.Exp: frozenset("L"), AF.Ln: frozenset("L")}.get(func)
        self.S.add("act", lambda e: e.activation(out=o, in_=i, func=func, scale=sc, bias=bi), reads, [out[1]],
                   dur=60 + (224 + n) / 1.2, tset=ts_, lab=str(func).split('.')[-1])

    def ts(self, out, in0, s1, s2, op0, op1=None, eng="dve"):
        reads = [in0[1]]
        a1 = self._sb(s1, reads)
        a2 = self._sb(s2, reads) if s2 is not None else None
        o, i = out[0], in0[0]
        n = 0 if o is None else o.shape[-1]
        d_ = 135 + (n + 151) / 0.96
        if op1 is None:
            self.S.add(eng, lambda e: e.tensor_scalar(out=o, in0=i, scalar1=a1, scalar2=None, op0=op0), reads, [out[1]], dur=d_)
        else:
            self.S.add(eng, lambda e: e.tensor_scalar(out=o, in0=i, scalar1=a1, scalar2=a2, op0=op0, op1=op1), reads, [out[1]], dur=d_)

    def stt(self, out, in0, sc, in1, op0, op1):
        reads = [in0[1], in1[1]]
        a = self._sb(sc, reads)
        o, i0, i1 = out[0], in0[0], in1[0]
        n = 0 if o is None else o.shape[-1]
        self.S.add("dve", lambda e: e.scalar_tensor_tensor(out=o, in0=i0, scalar=a, in1=i1, op0=op0, op1=op1), reads, [out[1]],
                   dur=135 + (n + 151) / 0.96)

    def tt(self, out, in0, in1, op):
        o, i0, i1 = out[0], in0[0], in1[0]
        n = 0 if o is None else o.shape[-1]
        self.S.add("dve", lambda e: e.tensor_tensor(out=o, in0=i0, in1=i1, op=op), [in0[1], in1[1]], [out[1]],
                   dur=135 + (n + 151) / 0.96)

    def ptt(self, out, in0, in1, op):
        o, i0, i1 = out[0], in0[0], in1[0]
        n = 0 if o is None else o.shape[-1]
        self.S.add("pool", lambda e: e.tensor_tensor(out=o, in0=i0, in1=i1, op=op), [in0[1], in1[1]], [out[1]],
                   dur=100 + 2.3 * n)

    def pcopy(self, out, in_):
        o, i_ = out[0], in_[0]
        n = 0 if o is None else o.shape[-1]
        self.S.add("pool", lambda e: e.tensor_copy(out=o, in_=i_), [in_[1]], [out[1]], dur=200 + 4.3 * n)

    def memset(self, out, val):
        o = out[0]
        self.S.add("dve", lambda e: e.memset(o, val), [], [out[1]], dur=200.0)

    def mm(self, bank, pairs, first=True, last=True):
        reads = []
        aps = []
        for (l, r) in pairs:
            reads.append(l[1])
            reads.append(r[1])
            aps.append((l[0], r[0]))
        ob = bank[0]
        n = len(aps)

        def emit(e):
            ins = None
            for i, (la, ra) in enumerate(aps):
                ins = e.matmul(ob, lhsT=la, rhs=ra, start=(first and i == 0), stop=(last and i == n - 1))
            return ins
        self.S.add("pe", emit, reads, [bank[1]], dur=256.0 * n)

    def wget(self, key, src, nk, ncols):
        slot = self.W.get(key, src, nk * ncols)
        return (slot, nk, ncols)

    def wl(self, slab, k, j):
        slot, nk, ncols = slab
        lo = k * ncols + j * 128
        return self.SLOT[slot].v(lo, lo + 128, 0, SLOT_COLS)

    def wtick(self):
        self.W.tick(self._load)

    def _load(self, key, slot):
        src, n = self.W.src_of[key]
        off = self.src_off[src]
        o = self.SLOT[slot].h[:, 0:n]
        i = self.wpack[:, off:off + n]
        sem = self.slot_sem[slot]
        self.S.add("pool", lambda e: [e.dma_start(out=o, in_=i)], [], [(self.SLOT[slot].name, 0, SLOT_COLS)],
                   sem=sem, inc=16, isdma=True, dur=3000.0 + n * 512 / 180.0)

    def norm(self, gcol, tiles, final=False):
        rst = {}
        banks = {}
        for t in tiles:
            bank = self.pb()
            banks[t] = bank
            for c in range(NC8):
                sq = self.sqb((t * NC8 + c) % 4)
                self.act(sq, self.xt(c, t), AF.Square)
                self.mm(bank, [(self.ONES.v(0, 128), sq)], first=(c == 0), last=(c == NC8 - 1))
        for t in tiles:
            r = self.tmp(3 + (t % 4))
            rst[t] = r
            self.act(r, banks[t], AF.Ln, scale=1.0 / D, bias=self.der(31))
        for t in tiles:
            self.act(rst[t], rst[t], AF.Exp, scale=-0.5)
        for t in tiles:
            for c in range(NC8):
                out = self.xt(c, t) if final else self.xn(c, t)
                self.stt(out, self.xt(c, t), self.vec(gcol + c), rst[t], ALU.mult, ALU.mult)

    def ffn(self, l, which):
        vb = l * VPL
        self.S.tag = "ffn%d.%d" % (which, l)
        self.norm(vb + (0 if which == 0 else 16), range(NT))
        groups = [(0, 3), (3, 3), (6, 3), (9, 2)]
        si = 0
        for half in range(2):
            for gi, (j0, nj) in enumerate(groups):
                ca = (half * 11 + j0) * 128
                wa = self.wget(("ua", l, which, half, gi), ("up", l, which, ca, nj * 128), 8, nj * 128)
                wb = self.wget(("ub", l, which, half, gi), ("up", l, which, DFF + ca, nj * 128), 8, nj * 128)
                for jl in range(nj):
                    jj = j0 + jl
                    for t in range(NT):
                        A = self.pb()
                        B = self.pb()
                        self.mm(A, [(self.wl(wa, k, jl), self.xn(k, t)) for k in range(NC8)])
                        self.mm(B, [(self.wl(wb, k, jl), self.xn(k, t)) for k in range(NC8)])
                        s = self.tmp(si % 3)
                        si += 1
                        self.act(s, A, AF.Silu)
                        hv = self.SCR.v(jj * S_LEN + t * TS, jj * S_LEN + (t + 1) * TS)
                        self.tt(hv, B, s, ALU.mult)
                self.wtick()
            tpasses = [range(NT)] if half == 0 else [range(0, 2), range(2, 4)]
            for pi, trange in enumerate(tpasses):
                for cg in range(4):
                    wd = self.wget(("wd", l, which, half, cg, pi), ("wd", l, which, half, cg * 256, 256), 11, 256)
                    for ci in range(2):
                        c = cg * 2 + ci
                        for t in trange:
                            Y = self.pb()
                            self.mm(Y, [(self.wl(wd, jj, ci), self.SCR.v(jj * S_LEN + t * TS, jj * S_LEN + (t + 1) * TS))
                                        for jj in range(11)])
                            self.stt(self.xt(c, t), Y, 0.5, self.xt(c, t), ALU.mult, ALU.add)
                    self.wtick()

    def derive(self, l):
        vb = l * VPL
        lam = self.vec(vb + 102, NH)
        d = lambda c: self.der(c, NH)
        c2, hba, hbx = d(0), d(10), d(20)
        t0, t1, t2, t3 = self.der(40, NH), self.der(50, NH), self.der(60, NH), self.der(70, NH)
        self.ts(t0, lam, -1.0, None, ALU.mult)
        self.tt(t1, lam, t0, ALU.max)
        self.act(t1, t1, AF.Exp, scale=-1.0)
        self.ts(t2, t1, 2.0, None, ALU.add)
        o2, i2 = t2[0], t2[0]
        self.S.add("dve", lambda e: e.reciprocal(out=o2, in_=i2), [t2[1]], [t2[1]], dur=400.0)
        self.tt(t2, t1, t2, ALU.mult)
        self.tt(t3, t2, t2, ALU.mult)
        self.ts(t1, t3, 1.0 / 11.0, 1.0 / 9.0, ALU.mult, ALU.add)
        for cst in (1.0 / 7.0, 1.0 / 5.0, 1.0 / 3.0, 1.0):
            self.tt(t1, t1, t3, ALU.mult)
            self.ts(t1, t1, cst, None, ALU.add)
        self.tt(t1, t1, t2, ALU.mult)
        self.ts(t0, t0, 0.0, None, ALU.max)
        self.stt(t1, t1, 2.0, t0, ALU.mult, ALU.add)
        self.ts(c2, t1, -4.0, None, ALU.mult)
        self.ts(hba, self.vec(vb + 82, NH), 0.5, None, ALU.mult)
        self.ts(hbx, self.vec(vb + 92, NH), 0.5, None, ALU.mult)

    def mixer(self, l):
        vb = l * VPL
        self.derive(l)
        self.memset(self.UC.v(0, 3 * NH), 0.0)
        self.memset(self.HC.v(0, NH), 0.0)
        self.memset(self.PC.v(0, 15 * NG), 0.0)
        PM0, HG0, M0 = 0, 4096, 14336
        for half in range(2):
            tiles = [2 * half, 2 * half + 1]
            self.S.tag = "mixnorm%d.%d" % (half, l)
            self.norm(vb + 8, tiles)
            self.S.tag = "m1.%d.%d" % (half, l)
            wp0 = self.wget(("winp0", l, half), ("win", l, 0, 384), 8, 384)
            wp1 = self.wget(("pool1", l, half), ("pool1", l), 12, 128)
            ubank = {}

            def m1a(g):
                slab, jl = (wp0, g) if g < 3 else (wp1, 0)
                pcv = self.PC.v(15 * g, 15 * g + 15)
                for ti, t in enumerate(tiles):
                    ui = 2 * (g % 2) + ti
                    U = self.pb()
                    self.mm(U, [(self.wl(slab, k, jl), self.xn(k, t)) for k in range(NC8)])
                    if ti == 0:
                        self.act(self.tmp(ui, 0, 15), pcv, AF.Copy)
                    else:
                        self.act(self.tmp(ui, 0, 15), self.tmp(ui - 1, 512, 527), AF.Copy)
                    self.act(self.tmp(ui, 15, 527), U, AF.Copy)
                    if ti == len(tiles) - 1:
                        self.act(pcv, self.tmp(ui, 512, 527), AF.Copy)

            def m1b(g):
                w = 2 << g
                for ti, t in enumerate(tiles):
                    ui = 2 * (g % 2) + ti
                    cur, ln, d, flip = ui, 527, 1, 0
                    while d < w:
                        nxt = 4 + ((ti * 2 + flip) % 4)
                        flip ^= 1
                        self.tt(self.tmp(nxt, 0, ln - d), self.tmp(cur, d, ln), self.tmp(cur, 0, ln - d), ALU.add)
                        cur, ln, d = nxt, ln - d, d * 2
                    o0 = 16 - w
                    pl = self.sqb(ti)
                    self.stt(pl, self.tmp(cur, o0, o0 + TS), 1.0 / w, self.tmp(ui, 15, 527), ALU.mult, ALU.subtract)
                    if t == 0:
                        nfix = w - 1
                        fx = self.tmp(8, 0, nfix)
                        self.tt(fx, self.tmp(cur, o0, o0 + nfix), self.INVC.v(0, nfix), ALU.mult)
                        self.tt(self.SQB.v(ti * TS, ti * TS + nfix, ti * TS, (ti + 1) * TS), fx,
                                self.tmp(ui, 15, 15 + nfix), ALU.subtract)
                    PMb = self.pb()
                    self.mm(PMb, [(self.wl(wp1, 8 + g, 0), pl)])
                    pmv = self.SCR.v(PM0 + g * 1024 + ti * TS, PM0 + g * 1024 + (ti + 1) * TS)
                    self.ts(pmv, PMb, self.vec(vb + 24 + g), self.vec(vb + 28 + g), ALU.add, ALU.mult)

            m1a(0)
            for g in range(NG):
                if g + 1 < NG:
                    m1a(g + 1)
                m1b(g)
            self.wtick()
            self.S.tag = "m2.%d.%d" % (half, l)
            nti = len(tiles)
            banks = {}

            xoff = 1024 if half == 0 else 0

            def xnf(j):
                lo = j * S_LEN + xoff
                ap = self.XN.h[:, lo:lo + 1024]
                return (ap.bitcast(F32) if ap is not None else None, ("XN", lo, lo + 1024))

            def sub(view, lo, hi):
                return (view[0][:, lo:hi] if view[0] is not None else None, view[1])

            def B_ut(h, ti, lo=0, hi=TS):
                return self.tmp(2 * (h % 2) + ti, lo, hi)

            def B_v(h, ti):
                return self.tmp(4 + 2 * (h % 2) + ti)

            def B_tra(h, ti):
                return self.tmp(8 + 2 * (h % 2) + ti)

            def B_tiq(h, ti):
                return self.scrf(2 * (h % 2) + ti)

            def B_a2m(h, ti):
                return self.scrf(4 + 2 * (h % 2) + ti)

            def B_hh(h, ti):
                return xnf(2 * (h % 2) + ti)

            def B_gl(h, ti):
                return xnf(4 + 2 * (h % 2) + ti)

            def phi1a(h, wu, wg, jl):
                ucv = self.UC.v(3 * h, 3 * h + 3)
                for ti, t in enumerate(tiles):
                    Ub = self.pb()
                    Gb = self.pb()
                    self.mm(Ub, [(self.wl(wu, k, 0), self.xn(k, t)) for k in range(NC8)])
                    self.mm(Gb, [(self.wl(wu, 8 + k, 0), self.xn(k, t)) for k in range(NC8)])
                    if ti == 0:
                        self.pcopy(B_ut(h, ti, 0, 3), ucv)
                    else:
                        self.pcopy(B_ut(h, ti, 0, 3), B_ut(h, ti - 1, 512, 515))
                    self.act(B_ut(h, ti, 3, 515), Ub, AF.Copy)
                    if ti == nti - 1:
                        self.pcopy(ucv, B_ut(h, ti, 512, 515))
                    self.act(B_gl(h, ti), Gb, AF.Gelu_apprx_tanh)
                for ti, t in enumerate(tiles):
                    v = B_v(h, ti)
                    cw = vb + 32 + h * 4
                    self.ts(v, B_ut(h, ti, 0, 512), self.vec(cw), self.vec(vb + 72 + h), ALU.mult, ALU.add)
                    for k in range(1, 4):
                        self.stt(v, B_ut(h, ti, k, k + 512), self.vec(cw + k), v, ALU.mult, ALU.add)

            def phi1b(h, wsm):
                for ti, t in enumerate(tiles):
                    vbf = self.sqb(2 * (h % 2) + ti)
                    self.act(vbf, B_v(h, ti), AF.Copy)
                    Rb = self.pb()
                    Ib = self.pb()
                    self.mm(Rb, [(self.wl(wsm, 16, 0), vbf)])
                    self.mm(Ib, [(self.wl(wsm, 17, 0), vbf)])
                    banks[(h, ti)] = (Rb, Ib)

            def a3t(h):
                for ti in range(nti):
                    Rb, Ib = banks[(h, ti)]
                    self.act(B_tra(h, ti), Rb, AF.Tanh, scale=0.5, bias=self.der(10 + h))
                    self.act(B_tiq(h, ti), Ib, AF.Tanh, scale=0.5, bias=self.der(20 + h))
                    self.stt(B_tiq(h, ti), B_tiq(h, ti), 1.0, B_v(h, ti), ALU.add, ALU.mult)

            def a3l(hs):
                for h in hs:
                    for ti in range(nti):
                        tra, a2m = B_tra(h, ti), B_a2m(h, ti)
                        self.act(tra, tra, AF.Exp, scale=self.der(h), bias=self.der(h))
                        self.act(a2m, tra, AF.Square)
                for h in hs:
                    for ti in range(nti):
                        a2m = B_a2m(h, ti)
                        self.act(a2m, a2m, AF.Ln, scale=-1.0, bias=self.der(30))
                for h in hs:
                    for ti in range(nti):
                        a2m = B_a2m(h, ti)
                        self.act(a2m, a2m, AF.Exp, scale=0.5, bias=self.der(32))

            def d2(h):
                hcv = self.HC.v(h, h + 1)
                for ti in range(nti):
                    tra, tiq, a2m, hh = B_tra(h, ti), B_tiq(h, ti), B_a2m(h, ti), B_hh(h, ti)
                    self.tt(tiq, tiq, a2m, ALU.mult)
                    iv = hcv if ti == 0 else sub(B_hh(h, ti - 1), 511, 512)
                    o, d0, d1, ini = hh[0], tra[0], tiq[0], iv[0]
                    self.S.add("dve", lambda e, o=o, d0=d0, d1=d1, ini=ini: e.tensor_tensor_scan(
                        out=o, data0=d0, data1=d1, initial=ini, op0=ALU.mult, op1=ALU.add),
                        [tra[1], tiq[1], iv[1]], [hh[1]], dur=1420.0)
                    if ti == nti - 1:
                        self.pcopy(hcv, sub(hh, 511, 512))
                    hgv = self.SCR.v(HG0 + h * 1024 + ti * TS, HG0 + h * 1024 + (ti + 1) * TS)
                    self.ptt(hgv, hh, B_gl(h, ti), ALU.mult)

            hslab = {}
            for i in range(NH + 1):
                if i < NH:
                    hslab[i] = self.wget(("head", l, half, i), ("head", l, i), 18, 128)
                    phi1a(i, hslab[i], None, 0)
                if i >= 1:
                    a3t(i - 1)
                    a3l([i - 1])
                if i < NH:
                    phi1b(i, hslab[i])
                    self.wtick()
                if i >= 1:
                    d2(i - 1)
            self.S.tag = "m3.%d.%d" % (half, l)
            for c in range(NC8):
                m3s = self.wget(("m3", l, half, c), ("m3", l, c), 20, 128)
                lu = self.wget(("lu", l, half, c // 2), ("lu", l, 256 * (c // 2), 256), NH, 256)
                for ti, t in enumerate(tiles):
                    GP, GL, YP, YL = self.pb(), self.pb(), self.pb(), self.pb()
                    self.mm(GP, [(self.wl(m3s, k, 0), self.xn(k, t)) for k in range(NC8)])
                    self.mm(GL, [(self.wl(m3s, 8 + k, 0), self.xn(k, t)) for k in range(NC8)])
                    self.mm(YP, [(self.wl(m3s, 16 + g, 0), self.SCR.v(PM0 + g * 1024 + ti * TS, PM0 + g * 1024 + (ti + 1) * TS))
                                 for g in range(NG)])
                    self.mm(YL, [(self.wl(lu, h, c % 2), self.SCR.v(HG0 + h * 1024 + ti * TS, HG0 + h * 1024 + (ti + 1) * TS))
                                 for h in range(NH)])
                    tgp, tgl = self.tmp(0 + ti), self.tmp(2 + ti)
                    self.act(tgp, GP, AF.Tanh, scale=0.5)
                    self.act(tgl, GL, AF.Tanh, scale=0.5)
                    self.stt(tgp, tgp, 1.0, YP, ALU.add, ALU.mult)
                    self.stt(tgl, tgl, 1.0, YL, ALU.add, ALU.mult)
                    mv = self.SCR.v(M0 + c * 1024 + ti * TS, M0 + c * 1024 + (ti + 1) * TS)
                    self.tt(mv, tgp, tgl, ALU.add)
                self.wtick()
            self.S.tag = "m4.%d.%d" % (half, l)
            for c in range(NC8):
                c3 = c // 3
                n3 = 3 if c3 < 2 else 2
                wo = self.wget(("wo", l, half, c3), ("wo", l, 384 * c3, n3 * 128), 8, n3 * 128)
                for ti, t in enumerate(tiles):
                    O = self.pb()
                    self.mm(O, [(self.wl(wo, k, c % 3), self.SCR.v(M0 + k * 1024 + ti * TS, M0 + k * 1024 + (ti + 1) * TS))
                                for k in range(NC8)])
                    self.stt(self.xt(c, t), O, 0.5, self.xt(c, t), ALU.mult, ALU.add)
                self.wtick()

    def body(self):
        self.pbi = 0
        S = self.S
        self.memset(self.ONES.v(0, 128), 1.0)
        for j in range(15):
            self.memset(self.INVC.v(j, j + 1), 1.0 / (j + 1))
        self.memset(self.der(30), 1.0)
        self.memset(self.der(31), EPS)
        self.memset(self.der(32), float(np.log(0.5)))
        xin, vin = self.xin, self.vin
        o = self.VEC.h[:, 0:self.nvec]
        S.add("sp", lambda e, o=o: [e.dma_start(out=o, in_=vin[:, :])], [], [("VEC", 0, self.nvec)],
              sem=self.s_in, inc=16, isdma=True, dur=4000.0)
        for c in range(NC8):
            o = self.XT.h[:, c * S_LEN:(c + 1) * S_LEN]
            i = xin[:, c * S_LEN:(c + 1) * S_LEN]
            S.add("sp", lambda e, o=o, i=i: [e.dma_start(out=o, in_=i)], [], [("XT", c * S_LEN, (c + 1) * S_LEN)],
                  sem=self.s_in, inc=16, isdma=True, dur=9000.0)
        self.wtick()
        for l in range(self.nl):
            self.ffn(l, 0)
            self.mixer(l)
            self.ffn(l, 1)
        self.norm(self.nl * VPL, range(NT), final=True)
        yout = self.yout
        for c in range(NC8):
            i = self.XT.h[:, c * S_LEN:(c + 1) * S_LEN]
            o = yout[:, c * S_LEN:(c + 1) * S_LEN]
            S.add("sp", lambda e, o=o, i=i: [e.dma_start(out=o, in_=i)], [("XT", c * S_LEN, (c + 1) * S_LEN)], [],
                  sem=self.s_out, inc=16, isdma=True, dur=9000.0)

    def build(self):
        nl = self.nl
        self.nvec = nl * VPL + 8
        self.S.dry = True
        self._alloc_dummy()
        self.body()
        self.W.finish_dry()
        self.src_off = {}
        off = 0
        self.src_list = []
        for key in self.W.order:
            src, n = self.W.src_of[key]
            if src not in self.src_off:
                self.src_off[src] = off
                self.src_list.append((src, off, n))
                off += n
        self.wtotal = off
        nc = bass.Bass("TRN2", target_bir_lowering=False)
        self.nc = nc
        self.xin = nc.dram_tensor("xin", [128, NC8 * S_LEN], F32, kind="ExternalInput").ap()
        self.vin = nc.dram_tensor("vin", [128, self.nvec], F32, kind="ExternalInput").ap()
        self.wpack = nc.dram_tensor("wpack", [128, self.wtotal], F32, kind="ExternalInput").ap()
        self.yout = nc.dram_tensor("yout", [128, NC8 * S_LEN], F32, kind="ExternalOutput").ap()
        with ExitStack() as es:
            def sb(name, cols, dt):
                return TT(name, es.enter_context(nc.sbuf_tensor(name, [128, cols], dt)))
            self.XT = sb("XT", NC8 * S_LEN, F32)
            self.XN = sb("XN", NC8 * S_LEN, BF16)
            self.SCR = sb("SCR", 11 * S_LEN, BF16)
            self.SLOT = [sb("SLOT%d" % i, SLOT_COLS, BF16) for i in range(NSLOT)]
            self.VEC = sb("VEC", self.nvec, F32)
            self.DER = sb("DER", 80, F32)
            self.ONES = sb("ONES", 128, BF16)
            self.TMP = sb("TMP", NTMP * TW, F32)
            self.SQB = sb("SQB", 4 * TS, BF16)
            self.UC = sb("UC", 3 * NH, F32)
            self.HC = sb("HC", NH, F32)
            self.PC = sb("PC", 15 * NG, F32)
            self.INVC = sb("INVC", 16, F32)
            self.P = [TT("PS%d" % i, es.enter_context(nc.psum_tensor("PS%d" % i, [128, TS], F32))) for i in range(8)]
            S = self.S
            S.dry = False
            S.reset()
            for e in Sched.ENGS:
                S.esem[e] = Sem(es.enter_context(nc.semaphore("sem_" + e)))
            self.slot_sem = [Sem(es.enter_context(nc.semaphore("sem_slot%d" % i))) for i in range(NSLOT)]
            self.s_in = Sem(es.enter_context(nc.semaphore("sem_in")))
            self.s_out = Sem(es.enter_context(nc.semaphore("sem_out")))
            self.W.reset_run()
            self.body()
            S.finalize(whole_sems=(self.s_in,))
            block = es.enter_context(nc.Block())
            s_out = self.s_out

            @block.tensor
            def _(e):
                S.emit_engine("pe", e)

            @block.scalar
            def _(e):
                S.emit_engine("act", e)

            @block.vector
            def _(e):
                S.emit_engine("dve", e)

            @block.gpsimd
            def _(e):
                S.emit_engine("pool", e)

            @block.sync
            def _(e):
                S.emit_engine("sp", e, tail=lambda eng: eng.wait_ge(s_out.h, s_out.total))
        return nc

    def _alloc_dummy(self):
        class _D:
            def __getitem__(self, k):
                return None
        dd = _D()
        mk = lambda n: TT(n, dd)
        self.XT, self.XN, self.SCR = mk("XT"), mk("XN"), mk("SCR")
        self.SLOT = [mk("SLOT%d" % i) for i in range(NSLOT)]
        self.VEC, self.DER, self.ONES, self.TMP, self.SQB = mk("VEC"), mk("DER"), mk("ONES"), mk("TMP"), mk("SQB")
        self.UC, self.HC, self.PC, self.INVC = mk("UC"), mk("HC"), mk("PC"), mk("INVC")
        self.P = [mk("PS%d" % i) for i in range(8)]
        self.xin = self.vin = self.yout = dd
        self.s_in = self.s_out = None
        self.slot_sem = [None] * NSLOT


_CACHE = {}


def _get_gen(nl):
    if nl not in _CACHE:
        g = Gen(nl)
        g.build()
        _CACHE[nl] = g
    return _CACHE[nl]


def kernel(_nl=4, **inputs):
    inp = {k: np.asarray(v) for k, v in inputs.items()}
    g = _get_gen(_nl)
    nc = g.nc
    wpack = np.empty((128, g.wtotal), np.float32)
    for (src, off, n) in g.src_list:
        wpack[:, off:off + n] = slab_source(src, inp)
    vecs = pack_vecs(inp, _nl)
    x = np.asarray(inp["x"], np.float32)
    B = x.shape[0]
    in_maps = []
    for b in range(B):
        xt = np.ascontiguousarray(x[b].T).reshape(NC8, 128, S_LEN).transpose(1, 0, 2).reshape(128, NC8 * S_LEN)
        in_maps.append({"xin": np.ascontiguousarray(xt), "vin": vecs, "wpack": wpack})
    res = run_bass_kernel_spmd(nc, in_maps, core_ids=list(range(B)))
    out = np.empty((B, S_LEN, D), np.float32)
    for b in range(B):
        y = np.asarray(res.results[b]["yout"]).reshape(128, NC8, S_LEN).transpose(1, 0, 2).reshape(D, S_LEN)
        out[b] = y.T
    return out
```

```python
import numpy as np
from contextlib import ExitStack
import concourse.bass as bass
import concourse.mybir as mybir
from concourse.bass_utils import run_bass_kernel_spmd

F32 = mybir.dt.float32
BF16 = mybir.dt.bfloat16
AF = mybir.ActivationFunctionType
ALU = mybir.AluOpType

D = 1024
S_LEN = 2048
NT = 4
TS = 512
NC8 = 8
DFF = 2816
NJ = 22
NH = 10
NG = 4
EPS = 1e-6
NSLOT = 5
SLOT_COLS = 3072
TW = 528
NTMP = 14
VPL = 112


class Sem:
    def __init__(self, h):
        self.h = h
        self.total = 0


class Ins:
    __slots__ = ("eng", "emit", "deps", "sig", "sem", "inc", "done", "isdma", "idx", "dur", "tset", "alld",
                 "succ", "nleft", "rt", "fin", "tag", "lab")

    def __init__(self, eng, emit, sem, inc, isdma):
        self.eng = eng
        self.emit = emit
        self.deps = []
        self.sig = isdma
        self.sem = sem
        self.inc = inc
        self.done = None
        self.isdma = isdma
        self.idx = 0
        self.dur = 500.0
        self.tset = None
        self.alld = []
        self.succ = []
        self.nleft = 0
        self.rt = 0.0
        self.fin = 0.0


class Sched:
    ENGS = ("pe", "act", "dve", "pool", "sp")

    def __init__(self):
        self.dry = True
        self.reset()

    def reset(self):
        self.all = []
        self.q = {e: [] for e in self.ENGS}
        self.recs = {}
        self.esem = {}

    def add(self, eng, emit, reads=(), writes=(), sem=None, inc=1, isdma=False, dur=500.0, tset=None, lab=""):
        if self.dry:
            return None
        if sem is None:
            sem = self.esem[eng]
        I = Ins(eng, emit, sem, inc, isdma)
        I.idx = len(self.all)
        I.tag = getattr(self, "tag", "")
        I.lab = lab
        I.dur = dur
        I.tset = tset
        deps = {}
        recs = self.recs
        for (tid, lo, hi) in reads:
            for r in recs.get(tid, ()):
                if r[0] < hi and lo < r[1] and r[2] is not None:
                    deps[id(r[2])] = (r[2], "RAW")
        for (tid, lo, hi) in writes:
            for r in recs.get(tid, ()):
                if r[0] < hi and lo < r[1]:
                    if r[2] is not None and id(r[2]) not in deps:
                        deps[id(r[2])] = (r[2], "WAW")
                    for rd in r[3]:
                        if id(rd) not in deps:
                            deps[id(rd)] = (rd, "WAR")
        for (tid, lo, hi) in reads:
            lst = recs.setdefault(tid, [])
            for r in lst:
                if r[0] == lo and r[1] == hi:
                    r[3].append(I)
                    break
            else:
                lst.append([lo, hi, None, [I]])
        for (tid, lo, hi) in writes:
            lst = recs.setdefault(tid, [])
            for r in lst:
                if r[0] == lo and r[1] == hi:
                    r[2] = I
                    r[3] = []
                    break
            else:
                lst.append([lo, hi, I, []])
        for (Dp, kind) in deps.values():
            if Dp is I:
                continue
            I.alld.append(Dp)
            if Dp.eng == eng and not Dp.isdma:
                if eng == "pe":
                    continue
                if kind != "RAW":
                    continue
            I.deps.append(Dp)
            Dp.sig = True
        self.all.append(I)
        self.q[eng].append(I)
        return I

    def schedule(self):
        HOP = 900.0
        SWITCH = 2700.0
        PATIENCE = 2500.0
        STICK = 100
        for I in self.all:
            I.nleft = len(I.alld)
            I.rt = 0.0
            for Dp in I.alld:
                Dp.succ.append(I)
        ready = {e: [] for e in self.ENGS}
        for I in self.all:
            if I.nleft == 0:
                ready[I.eng].append(I)
        free = {e: 0.0 for e in self.ENGS}
        order = {e: [] for e in self.ENGS}
        cur_set = None
        left = len(self.all)
        while left:
            best_e, best_t = None, None
            for e in self.ENGS:
                r = ready[e]
                if not r:
                    continue
                t = max(free[e], min(x.rt for x in r))
                if best_t is None or t < best_t:
                    best_e, best_t = e, t
            e, t = best_e, best_t
            r = ready[e]
            cands = [x for x in r if x.rt <= t]
            I = None
            if e == "act":
                oldest = min(x.idx for x in cands)
                same = [x for x in r if (x.tset is None or cur_set in x.tset) and x.rt <= t + PATIENCE]
                if same:
                    so = min(same, key=lambda x: (max(x.rt, t), x.idx))
                    if so.idx - oldest < STICK:
                        I = so
                        t = max(t, so.rt)
            if I is None:
                I = min(cands, key=lambda x: x.idx)
            r.remove(I)
            start = t
            if e == "act" and I.tset is not None and cur_set not in I.tset:
                start += SWITCH
                cur_set = sorted(I.tset)[0]
            if I.isdma:
                free[e] = start + 900.0
                I.fin = start + I.dur
            else:
                I.fin = start + I.dur
                free[e] = I.fin
            order[e].append(I)
            left -= 1
            for sc in I.succ:
                lat = I.fin + (HOP if (sc.eng != e or I.isdma) else 0.0)
                if lat > sc.rt:
                    sc.rt = lat
                sc.nleft -= 1
                if sc.nleft == 0:
                    ready[sc.eng].append(sc)
        self.q = order
        self.sim_end = max(free.values())

    def finalize(self, whole_sems=(), reorder=True):
        if reorder:
            self.schedule()
        for e in self.ENGS:
            for I in self.q[e]:
                if I.sig:
                    I.sem.total += I.inc
                    I.done = I.sem.total
        for I in self.all:
            if I.sig and any(I.sem is w for w in whole_sems):
                I.done = I.sem.total

    def emit_engine(self, engname, eng, tail=None):
        waited = {}
        for I in self.q[engname]:
            need = {}
            for Dp in I.deps:
                k = id(Dp.sem)
                if need.get(k, (None, 0))[1] < Dp.done:
                    need[k] = (Dp.sem, Dp.done)
            for k, (sem, val) in need.items():
                if waited.get(k, 0) < val:
                    eng.wait_ge(sem.h, val)
                    waited[k] = val
            r = I.emit(eng)
            if I.sig:
                if isinstance(r, list):
                    for x in r:
                        x.then_inc(I.sem.h, 16)
                else:
                    r.then_inc(I.sem.h, I.inc)
        if tail is not None:
            tail(eng)


class TT:
    def __init__(self, name, h):
        self.name = name
        self.h = h

    def v(self, lo, hi, rlo=None, rhi=None):
        return (self.h[:, lo:hi], (self.name, lo if rlo is None else rlo, hi if rhi is None else rhi))


class WStream:
    def __init__(self, S):
        self.S = S
        self.uses = []
        self.order = []
        self.last_pos = {}
        self.src_of = {}
        self.reset_run()

    def reset_run(self):
        self.pos = 0
        self.nloaded = 0
        self.slot_of = {}

    def finish_dry(self):
        seen = set()
        for i, k in enumerate(self.uses):
            if k not in seen:
                seen.add(k)
                self.order.append(k)
            self.last_pos[k] = i

    def get(self, key, src, n):
        if self.S.dry:
            self.uses.append(key)
            self.src_of[key] = (src, n)
            return 0
        assert self.uses[self.pos] == key
        self.pos += 1
        assert key in self.slot_of, ("slab not resident", key)
        idx = self.order.index(key) if False else self.slot_of[key]
        return idx

    def tick(self, loader):
        if self.S.dry:
            return
        while self.nloaded < len(self.order):
            n = self.nloaded
            if n >= NSLOT:
                old = self.order[n - NSLOT]
                if self.last_pos[old] >= self.pos:
                    break
                self.slot_of.pop(old, None)
            key = self.order[n]
            slot = n % NSLOT
            loader(key, slot)
            self.slot_of[key] = slot
            self.nloaded += 1


def _rows_slab(W, r0, nk, c0, ncols):
    blk = W[r0:r0 + nk * 128, c0:c0 + ncols].reshape(nk, 128, ncols)
    return np.ascontiguousarray(blk.transpose(1, 0, 2)).reshape(128, nk * ncols)


def slab_source(src, inp):
    kind = src[0]
    l = src[1]
    if kind == "up":
        W = inp["ffn1_w_up" if src[2] == 0 else "ffn2_w_up"][l]
        return _rows_slab(W, 0, 8, src[3], src[4])
    if kind == "wd":
        W = inp["ffn1_w_down" if src[2] == 0 else "ffn2_w_down"][l]
        return _rows_slab(W, src[3] * 11 * 128, 11, src[4], src[5])
    if kind == "win":
        return _rows_slab(inp["w_in"][l], 0, 8, src[2], src[3])
    if kind == "small":
        pw = inp["pool_w"][l]
        wa = inp["lru_w_a"][l]
        wx = inp["lru_w_x"][l]
        allw = np.concatenate([pw, wa, wx], axis=0)
        return np.ascontiguousarray(allw.transpose(1, 0, 2)).reshape(128, 24 * 128)
    if kind == "head":
        h = src[2]
        W = inp["w_in"][l]
        blks = [W[k * 128:(k + 1) * 128, 512 + 128 * h:512 + 128 * (h + 1)] for k in range(8)]
        blks += [W[k * 128:(k + 1) * 128, 1792 + 128 * h:1792 + 128 * (h + 1)] for k in range(8)]
        blks += [inp["lru_w_a"][l][h], inp["lru_w_x"][l][h]]
        return np.ascontiguousarray(np.stack(blks, axis=1)).reshape(128, 18 * 128)
    if kind == "pool1":
        W = inp["w_in"][l]
        blks = [W[k * 128:(k + 1) * 128, 384:512] for k in range(8)] + [inp["pool_w"][l][g] for g in range(4)]
        return np.ascontiguousarray(np.stack(blks, axis=1)).reshape(128, 12 * 128)
    if kind == "m3":
        c = src[2]
        W = inp["w_in"][l]
        blks = [W[k * 128:(k + 1) * 128, 3072 + 128 * c:3072 + 128 * (c + 1)] for k in range(8)]
        blks += [W[k * 128:(k + 1) * 128, 4096 + 128 * c:4096 + 128 * (c + 1)] for k in range(8)]
        PU = inp["w_pool_up"][l]
        blks += [PU[g * 128:(g + 1) * 128, 128 * c:128 * (c + 1)] for g in range(4)]
        return np.ascontiguousarray(np.stack(blks, axis=1)).reshape(128, 20 * 128)
    if kind == "pu":
        return _rows_slab(inp["w_pool_up"][l], 0, 4, src[2], src[3])
    if kind == "lu":
        return _rows_slab(inp["w_lru_up"][l], 0, 10, src[2], src[3])
    if kind == "wo":
        return _rows_slab(inp["w_out"][l], 0, 8, src[2], src[3])
    raise KeyError(src)


def pack_vecs(inp, nl):
    cols = []

    def fm(v, nchunk):
        return np.asarray(v, np.float32).reshape(nchunk, 128).T

    for l in range(nl):
        cols.append(fm(inp["norm_ffn1"][l], 8))
        cols.append(fm(inp["norm_mix"][l], 8))
        cols.append(fm(inp["norm_ffn2"][l], 8))
        cols.append(fm(inp["pool_b"][l].reshape(-1), 4))
        cols.append(fm(inp["pool_scale"][l], 4))
        cw = np.asarray(inp["conv_w"][l], np.float32)
        cwf = cw.reshape(4, NH, 128).transpose(2, 1, 0).reshape(128, NH * 4)
        cols.append(cwf)
        cols.append(fm(inp["conv_b"][l], NH))
        cols.append(fm(inp["lru_b_a"][l].reshape(-1), NH))
        cols.append(fm(inp["lru_b_x"][l].reshape(-1), NH))
        cols.append(fm(inp["lru_lambda"][l], NH))
    cols.append(fm(inp["final_norm"], 8))
    return np.ascontiguousarray(np.concatenate(cols, axis=1), dtype=np.float32)


class Gen:
    def __init__(self, nl):
        self.nl = nl
        self.S = Sched()
        self.W = WStream(self.S)
        self.pbi = 0

    def pb(self):
        b = self.P[self.pbi % 8]
        self.pbi += 1
        return b.v(0, TS)

    def xt(self, c, t):
        return self.XT.v(c * S_LEN + t * TS, c * S_LEN + (t + 1) * TS)

    def xn(self, c, t):
        return self.XN.v(c * S_LEN + t * TS, c * S_LEN + (t + 1) * TS)

    def tmp(self, i, lo=0, hi=TS):
        return self.TMP.v(i * TW + lo, i * TW + hi, i * TW, (i + 1) * TW)

    def scrf(self, j):
        lo = 14336 + j * 1024
        ap = self.SCR.h[:, lo:lo + 1024]
        return (ap.bitcast(F32) if ap is not None else None, ("SCR", lo, lo + 1024))

    def sqb(self, i):
        return self.SQB.v(i * TS, (i + 1) * TS)

    def vec(self, col, n=1):
        return self.VEC.v(col, col + n, 0, self.nvec)

    def der(self, col, n=1):
        return self.DER.v(col, col + n)

    @staticmethod
    def _sb(x, reads):
        if isinstance(x, tuple):
            reads.append(x[1])
            return x[0]
        return x

    def act(self, out, in_, func, scale=1.0, bias=0.0):
        reads = [in_[1]]
        sc = self._sb(scale, reads)
        bi = self._sb(bias, reads)
        o, i = out[0], in_[0]
        n = 0 if o is None else o.shape[-1]
        ts_ = {AF.Silu: frozenset("S"), AF.Gelu_apprx_tanh: frozenset("G"), AF.Tanh: frozenset("GS"),
               AF.Exp: frozenset("L"), AF.Ln: frozenset("L")}.get(func)
        self.S.add("act", lambda e: e.activation(out=o, in_=i, func=func, scale=sc, bias=bi), reads, [out[1]],
                   dur=60 + (224 + n) / 1.2, tset=ts_, lab=str(func).split('.')[-1])

    def ts(self, out, in0, s1, s2, op0, op1=None, eng="dve"):
        reads = [in0[1]]
        a1 = self._sb(s1, reads)
        a2 = self._sb(s2, reads) if s2 is not None else None
        o, i = out[0], in0[0]
        n = 0 if o is None else o.shape[-1]
        d_ = 135 + (n + 151) / 0.96
        if op1 is None:
            self.S.add(eng, lambda e: e.tensor_scalar(out=o, in0=i, scalar1=a1, scalar2=None, op0=op0), reads, [out[1]], dur=d_)
        else:
            self.S.add(eng, lambda e: e.tensor_scalar(out=o, in0=i, scalar1=a1, scalar2=a2, op0=op0, op1=op1), reads, [out[1]], dur=d_)

    def stt(self, out, in0, sc, in1, op0, op1):
        reads = [in0[1], in1[1]]
        a = self._sb(sc, reads)
        o, i0, i1 = out[0], in0[0], in1[0]
        n = 0 if o is None else o.shape[-1]
        self.S.add("dve", lambda e: e.scalar_tensor_tensor(out=o, in0=i0, scalar=a, in1=i1, op0=op0, op1=op1), reads, [out[1]],
                   dur=135 + (n + 151) / 0.96)

    def tt(self, out, in0, in1, op):
        o, i0, i1 = out[0], in0[0], in1[0]
        n = 0 if o is None else o.shape[-1]
        self.S.add("dve", lambda e: e.tensor_tensor(out=o, in0=i0, in1=i1, op=op), [in0[1], in1[1]], [out[1]],
                   dur=135 + (n + 151) / 0.96)

    def ptt(self, out, in0, in1, op):
        o, i0, i1 = out[0], in0[0], in1[0]
        n = 0 if o is None else o.shape[-1]
        self.S.add("pool", lambda e: e.tensor_tensor(out=o, in0=i0, in1=i1, op=op), [in0[1], in1[1]], [out[1]],
                   dur=100 + 2.3 * n)

    def pcopy(self, out, in_):
        o, i_ = out[0], in_[0]
        n = 0 if o is None else o.shape[-1]
        self.S.add("pool", lambda e: e.tensor_copy(out=o, in_=i_), [in_[1]], [out[1]], dur=200 + 4.3 * n)

    def memset(self, out, val):
        o = out[0]
        self.S.add("dve", lambda e: e.memset(o, val), [], [out[1]], dur=200.0)

    def mm(self, bank, pairs, first=True, last=True):
        reads = []
        aps = []
        for (l, r) in pairs:
            reads.append(l[1])
            reads.append(r[1])
            aps.append((l[0], r[0]))
        ob = bank[0]
        n = len(aps)

        def emit(e):
            ins = None
            for i, (la, ra) in enumerate(aps):
                ins = e.matmul(ob, lhsT=la, rhs=ra, start=(first and i == 0), stop=(last and i == n - 1))
            return ins
        self.S.add("pe", emit, reads, [bank[1]], dur=256.0 * n)

    def wget(self, key, src, nk, ncols):
        slot = self.W.get(key, src, nk * ncols)
        return (slot, nk, ncols)

    def wl(self, slab, k, j):
        slot, nk, ncols = slab
        lo = k * ncols + j * 128
        return self.SLOT[slot].v(lo, lo + 128, 0, SLOT_COLS)

    def wtick(self):
        self.W.tick(self._load)

    def _load(self, key, slot):
        src, n = self.W.src_of[key]
        off = self.src_off[src]
        o = self.SLOT[slot].h[:, 0:n]
        i = self.wpack[:, off:off + n]
        sem = self.slot_sem[slot]
        self.S.add("pool", lambda e: [e.dma_start(out=o, in_=i)], [], [(self.SLOT[slot].name, 0, SLOT_COLS)],
                   sem=sem, inc=16, isdma=True, dur=3000.0 + n * 512 / 180.0)

    def norm(self, gcol, tiles, final=False):
        for t in tiles:
            bank = self.pb()
            for c in range(NC8):
                sq = self.sqb((t * NC8 + c) % 4)
                self.act(sq, self.xt(c, t), AF.Square)
                self.mm(bank, [(self.ONES.v(0, 128), sq)], first=(c == 0), last=(c == NC8 - 1))
            r = self.tmp(3 + (t % 4))
            self.act(r, bank, AF.Ln, scale=1.0 / D, bias=self.der(31))
            self.act(r, r, AF.Exp, scale=-0.5)
            for c in range(NC8):
                out = self.xt(c, t) if final else self.xn(c, t)
                self.stt(out, self.xt(c, t), self.vec(gcol + c), r, ALU.mult, ALU.mult)

    def ffn(self, l, which):
        vb = l * VPL
        self.S.tag = "ffn%d.%d" % (which, l)
        self.norm(vb + (0 if which == 0 else 16), range(NT))
        groups = [(0, 3), (3, 3), (6, 3), (9, 2)]
        si = 0
        for half in range(2):
            for gi, (j0, nj) in enumerate(groups):
                ca = (half * 11 + j0) * 128
                wa = self.wget(("ua", l, which, half, gi), ("up", l, which, ca, nj * 128), 8, nj * 128)
                wb = self.wget(("ub", l, which, half, gi), ("up", l, which, DFF + ca, nj * 128), 8, nj * 128)
                for jl in range(nj):
                    jj = j0 + jl
                    for t in range(NT):
                        A = self.pb()
                        B = self.pb()
                        self.mm(A, [(self.wl(wa, k, jl), self.xn(k, t)) for k in range(NC8)])
                        self.mm(B, [(self.wl(wb, k, jl), self.xn(k, t)) for k in range(NC8)])
                        s = self.tmp(si % 3)
                        si += 1
                        self.act(s, A, AF.Silu)
                        hv = self.SCR.v(jj * S_LEN + t * TS, jj * S_LEN + (t + 1) * TS)
                        self.tt(hv, B, s, ALU.mult)
                self.wtick()
            tpasses = [range(NT)] if half == 0 else [range(0, 2), range(2, 4)]
            for pi, trange in enumerate(tpasses):
                for cg in range(4):
                    wd = self.wget(("wd", l, which, half, cg, pi), ("wd", l, which, half, cg * 256, 256), 11, 256)
                    for ci in range(2):
                        c = cg * 2 + ci
                        for t in trange:
                            Y = self.pb()
                            self.mm(Y, [(self.wl(wd, jj, ci), self.SCR.v(jj * S_LEN + t * TS, jj * S_LEN + (t + 1) * TS))
                                        for jj in range(11)])
                            self.stt(self.xt(c, t), Y, 0.5, self.xt(c, t), ALU.mult, ALU.add)
                    self.wtick()

    def derive(self, l):
        vb = l * VPL
        lam = self.vec(vb + 102, NH)
        d = lambda c: self.der(c, NH)
        c2, hba, hbx = d(0), d(10), d(20)
        t0, t1, t2, t3 = self.der(40, NH), self.der(50, NH), self.der(60, NH), self.der(70, NH)
        self.ts(t0, lam, -1.0, None, ALU.mult)
        self.tt(t1, lam, t0, ALU.max)
        self.act(t1, t1, AF.Exp, scale=-1.0)
        self.ts(t2, t1, 2.0, None, ALU.add)
        o2, i2 = t2[0], t2[0]
        self.S.add("dve", lambda e: e.reciprocal(out=o2, in_=i2), [t2[1]], [t2[1]], dur=400.0)
        self.tt(t2, t1, t2, ALU.mult)
        self.tt(t3, t2, t2, ALU.mult)
        self.ts(t1, t3, 1.0 / 11.0, 1.0 / 9.0, ALU.mult, ALU.add)
        for cst in (1.0 / 7.0, 1.0 / 5.0, 1.0 / 3.0, 1.0):
            self.tt(t1, t1, t3, ALU.mult)
            self.ts(t1, t1, cst, None, ALU.add)
        self.tt(t1, t1, t2, ALU.mult)
        self.ts(t0, t0, 0.0, None, ALU.max)
        self.stt(t1, t1, 2.0, t0, ALU.mult, ALU.add)
        self.ts(c2, t1, -4.0, None, ALU.mult)
        self.ts(hba, self.vec(vb + 82, NH), 0.5, None, ALU.mult)
        self.ts(hbx, self.vec(vb + 92, NH), 0.5, None, ALU.mult)

    def mixer(self, l):
        vb = l * VPL
        self.derive(l)
        self.memset(self.UC.v(0, 3 * NH), 0.0)
        self.memset(self.HC.v(0, NH), 0.0)
        self.memset(self.PC.v(0, 15 * NG), 0.0)
        PM0, HG0, M0 = 0, 4096, 14336
        for half in range(2):
            tiles = [2 * half, 2 * half + 1]
            self.S.tag = "mixnorm%d.%d" % (half, l)
            self.norm(vb + 8, tiles)
            self.S.tag = "m1.%d.%d" % (half, l)
            wp0 = self.wget(("winp0", l, half), ("win", l, 0, 384), 8, 384)
            wp1 = self.wget(("pool1", l, half), ("pool1", l), 12, 128)
            ubank = {}

            def m1a(g):
                slab, jl = (wp0, g) if g < 3 else (wp1, 0)
                pcv = self.PC.v(15 * g, 15 * g + 15)
                for ti, t in enumerate(tiles):
                    ui = 2 * (g % 2) + ti
                    U = self.pb()
                    self.mm(U, [(self.wl(slab, k, jl), self.xn(k, t)) for k in range(NC8)])
                    if ti == 0:
                        self.act(self.tmp(ui, 0, 15), pcv, AF.Copy)
                    else:
                        self.act(self.tmp(ui, 0, 15), self.tmp(ui - 1, 512, 527), AF.Copy)
                    self.act(self.tmp(ui, 15, 527), U, AF.Copy)
                    if ti == len(tiles) - 1:
                        self.act(pcv, self.tmp(ui, 512, 527), AF.Copy)

            def m1b(g):
                w = 2 << g
                for ti, t in enumerate(tiles):
                    ui = 2 * (g % 2) + ti
                    cur, ln, d, flip = ui, 527, 1, 0
                    while d < w:
                        nxt = 4 + ((ti * 2 + flip) % 4)
                        flip ^= 1
                        self.tt(self.tmp(nxt, 0, ln - d), self.tmp(cur, d, ln), self.tmp(cur, 0, ln - d), ALU.add)
                        cur, ln, d = nxt, ln - d, d * 2
                    o0 = 16 - w
                    pl = self.sqb(ti)
                    self.stt(pl, self.tmp(cur, o0, o0 + TS), 1.0 / w, self.tmp(ui, 15, 527), ALU.mult, ALU.subtract)
                    if t == 0:
                        nfix = w - 1
                        fx = self.tmp(8, 0, nfix)
                        self.tt(fx, self.tmp(cur, o0, o0 + nfix), self.INVC.v(0, nfix), ALU.mult)
                        self.tt(self.SQB.v(ti * TS, ti * TS + nfix, ti * TS, (ti + 1) * TS), fx,
                                self.tmp(ui, 15, 15 + nfix), ALU.subtract)
                    PMb = self.pb()
                    self.mm(PMb, [(self.wl(wp1, 8 + g, 0), pl)])
                    pmv = self.SCR.v(PM0 + g * 1024 + ti * TS, PM0 + g * 1024 + (ti + 1) * TS)
                    self.ts(pmv, PMb, self.vec(vb + 24 + g), self.vec(vb + 28 + g), ALU.add, ALU.mult)

            m1a(0)
            for g in range(NG):
                if g + 1 < NG:
                    m1a(g + 1)
                m1b(g)
            self.wtick()
            self.S.tag = "m2.%d.%d" % (half, l)
            nti = len(tiles)
            banks = {}

            xoff = 1024 if half == 0 else 0

            def xnf(j):
                lo = j * S_LEN + xoff
                ap = self.XN.h[:, lo:lo + 1024]
                return (ap.bitcast(F32) if ap is not None else None, ("XN", lo, lo + 1024))

            def sub(view, lo, hi):
                return (view[0][:, lo:hi] if view[0] is not None else None, view[1])

            def B_ut(h, ti, lo=0, hi=TS):
                return self.tmp(2 * (h % 2) + ti, lo, hi)

            def B_v(h, ti):
                return self.tmp(4 + 2 * (h % 2) + ti)

            def B_tra(h, ti):
                return self.tmp(8 + 2 * (h % 2) + ti)

            def B_tiq(h, ti):
                return self.scrf(2 * (h % 2) + ti)

            def B_a2m(h, ti):
                return self.scrf(4 + 2 * (h % 2) + ti)

            def B_hh(h, ti):
                return xnf(2 * (h % 2) + ti)

            def B_gl(h, ti):
                return xnf(4 + 2 * (h % 2) + ti)

            def phi1a(h, wu, wg, jl):
                ucv = self.UC.v(3 * h, 3 * h + 3)
                for ti, t in enumerate(tiles):
                    Ub = self.pb()
                    Gb = self.pb()
                    self.mm(Ub, [(self.wl(wu, k, 0), self.xn(k, t)) for k in range(NC8)])
                    self.mm(Gb, [(self.wl(wu, 8 + k, 0), self.xn(k, t)) for k in range(NC8)])
                    if ti == 0:
                        self.pcopy(B_ut(h, ti, 0, 3), ucv)
                    else:
                        self.pcopy(B_ut(h, ti, 0, 3), B_ut(h, ti - 1, 512, 515))
                    self.act(B_ut(h, ti, 3, 515), Ub, AF.Copy)
                    if ti == nti - 1:
                        self.pcopy(ucv, B_ut(h, ti, 512, 515))
                    self.act(B_gl(h, ti), Gb, AF.Gelu_apprx_tanh)
                for ti, t in enumerate(tiles):
                    v = B_v(h, ti)
                    cw = vb + 32 + h * 4
                    self.ts(v, B_ut(h, ti, 0, 512), self.vec(cw), self.vec(vb + 72 + h), ALU.mult, ALU.add)
                    for k in range(1, 4):
                        self.stt(v, B_ut(h, ti, k, k + 512), self.vec(cw + k), v, ALU.mult, ALU.add)

            def phi1b(h, wsm):
                for ti, t in enumerate(tiles):
                    vbf = self.sqb(2 * (h % 2) + ti)
                    self.act(vbf, B_v(h, ti), AF.Copy)
                    Rb = self.pb()
                    Ib = self.pb()
                    self.mm(Rb, [(self.wl(wsm, 16, 0), vbf)])
                    self.mm(Ib, [(self.wl(wsm, 17, 0), vbf)])
                    banks[(h, ti)] = (Rb, Ib)

            def a3t(h):
                for ti in range(nti):
                    Rb, Ib = banks[(h, ti)]
                    self.act(B_tra(h, ti), Rb, AF.Tanh, scale=0.5, bias=self.der(10 + h))
                    self.act(B_tiq(h, ti), Ib, AF.Tanh, scale=0.5, bias=self.der(20 + h))
                    self.stt(B_tiq(h, ti), B_tiq(h, ti), 1.0, B_v(h, ti), ALU.add, ALU.mult)

            def a3l(hs):
                for h in hs:
                    for ti in range(nti):
                        tra, a2m = B_tra(h, ti), B_a2m(h, ti)
                        self.act(tra, tra, AF.Exp, scale=self.der(h), bias=self.der(h))
                        self.act(a2m, tra, AF.Square)
                for h in hs:
                    for ti in range(nti):
                        a2m = B_a2m(h, ti)
                        self.act(a2m, a2m, AF.Ln, scale=-1.0, bias=self.der(30))
                for h in hs:
                    for ti in range(nti):
                        a2m = B_a2m(h, ti)
                        self.act(a2m, a2m, AF.Exp, scale=0.5, bias=self.der(32))

            def d2(h):
                hcv = self.HC.v(h, h + 1)
                for ti in range(nti):
                    tra, tiq, a2m, hh = B_tra(h, ti), B_tiq(h, ti), B_a2m(h, ti), B_hh(h, ti)
                    self.tt(tiq, tiq, a2m, ALU.mult)
                    iv = hcv if ti == 0 else sub(B_hh(h, ti - 1), 511, 512)
                    o, d0, d1, ini = hh[0], tra[0], tiq[0], iv[0]
                    self.S.add("dve", lambda e, o=o, d0=d0, d1=d1, ini=ini: e.tensor_tensor_scan(
                        out=o, data0=d0, data1=d1, initial=ini, op0=ALU.mult, op1=ALU.add),
                        [tra[1], tiq[1], iv[1]], [hh[1]], dur=1420.0)
                    if ti == nti - 1:
                        self.pcopy(hcv, sub(hh, 511, 512))
                    hgv = self.SCR.v(HG0 + h * 1024 + ti * TS, HG0 + h * 1024 + (ti + 1) * TS)
                    self.ptt(hgv, hh, B_gl(h, ti), ALU.mult)

            hslab = {}
            for i in range(NH + 1):
                if i < NH:
                    hslab[i] = self.wget(("head", l, half, i), ("head", l, i), 18, 128)
                    phi1a(i, hslab[i], None, 0)
                if i >= 1:
                    a3t(i - 1)
                    a3l([i - 1])
                if i < NH:
                    phi1b(i, hslab[i])
                    self.wtick()
                if i >= 1:
                    d2(i - 1)
            self.S.tag = "m3.%d.%d" % (half, l)
            for c in range(NC8):
                m3s = self.wget(("m3", l, half, c), ("m3", l, c), 20, 128)
                lu = self.wget(("lu", l, half, c // 2), ("lu", l, 256 * (c // 2), 256), NH, 256)
                for ti, t in enumerate(tiles):
                    GP, GL, YP, YL = self.pb(), self.pb(), self.pb(), self.pb()
                    self.mm(GP, [(self.wl(m3s, k, 0), self.xn(k, t)) for k in range(NC8)])
                    self.mm(GL, [(self.wl(m3s, 8 + k, 0), self.xn(k, t)) for k in range(NC8)])
                    self.mm(YP, [(self.wl(m3s, 16 + g, 0), self.SCR.v(PM0 + g * 1024 + ti * TS, PM0 + g * 1024 + (ti + 1) * TS))
                                 for g in range(NG)])
                    self.mm(YL, [(self.wl(lu, h, c % 2), self.SCR.v(HG0 + h * 1024 + ti * TS, HG0 + h * 1024 + (ti + 1) * TS))
                                 for h in range(NH)])
                    tgp, tgl = self.tmp(0 + ti), self.tmp(2 + ti)
                    self.act(tgp, GP, AF.Tanh, scale=0.5)
                    self.act(tgl, GL, AF.Tanh, scale=0.5)
                    self.stt(tgp, tgp, 1.0, YP, ALU.add, ALU.mult)
                    self.stt(tgl, tgl, 1.0, YL, ALU.add, ALU.mult)
                    mv = self.SCR.v(M0 + c * 1024 + ti * TS, M0 + c * 1024 + (ti + 1) * TS)
                    self.tt(mv, tgp, tgl, ALU.add)
                self.wtick()
            self.S.tag = "m4.%d.%d" % (half, l)
            for c in range(NC8):
                c3 = c // 3
                n3 = 3 if c3 < 2 else 2
                wo = self.wget(("wo", l, half, c3), ("wo", l, 384 * c3, n3 * 128), 8, n3 * 128)
                for ti, t in enumerate(tiles):
                    O = self.pb()
                    self.mm(O, [(self.wl(wo, k, c % 3), self.SCR.v(M0 + k * 1024 + ti * TS, M0 + k * 1024 + (ti + 1) * TS))
                                for k in range(NC8)])
                    self.stt(self.xt(c, t), O, 0.5, self.xt(c, t), ALU.mult, ALU.add)
                self.wtick()

    def body(self):
        self.pbi = 0
        S = self.S
        self.memset(self.ONES.v(0, 128), 1.0)
        for j in range(15):
            self.memset(self.INVC.v(j, j + 1), 1.0 / (j + 1))
        self.memset(self.der(30), 1.0)
        self.memset(self.der(31), EPS)
        self.memset(self.der(32), float(np.log(0.5)))
        xin, vin = self.xin, self.vin
        o = self.VEC.h[:, 0:self.nvec]
        S.add("sp", lambda e, o=o: [e.dma_start(out=o, in_=vin[:, :])], [], [("VEC", 0, self.nvec)],
              sem=self.s_in, inc=16, isdma=True, dur=4000.0)
        for c in range(NC8):
            o = self.XT.h[:, c * S_LEN:(c + 1) * S_LEN]
            i = xin[:, c * S_LEN:(c + 1) * S_LEN]
            S.add("sp", lambda e, o=o, i=i: [e.dma_start(out=o, in_=i)], [], [("XT", c * S_LEN, (c + 1) * S_LEN)],
                  sem=self.s_in, inc=16, isdma=True, dur=9000.0)
        self.wtick()
        for l in range(self.nl):
            self.ffn(l, 0)
            self.mixer(l)
            self.ffn(l, 1)
        self.norm(self.nl * VPL, range(NT), final=True)
        yout = self.yout
        for c in range(NC8):
            i = self.XT.h[:, c * S_LEN:(c + 1) * S_LEN]
            o = yout[:, c * S_LEN:(c + 1) * S_LEN]
            S.add("sp", lambda e, o=o, i=i: [e.dma_start(out=o, in_=i)], [("XT", c * S_LEN, (c + 1) * S_LEN)], [],
                  sem=self.s_out, inc=16, isdma=True, dur=9000.0)

    def build(self):
        nl = self.nl
        self.nvec = nl * VPL + 8
        self.S.dry = True
        self._alloc_dummy()
        self.body()
        self.W.finish_dry()
        self.src_off = {}
        off = 0
        self.src_list = []
        for key in self.W.order:
            src, n = self.W.src_of[key]
            if src not in self.src_off:
                self.src_off[src] = off
                self.src_list.append((src, off, n))
                off += n
        self.wtotal = off
        nc = bass.Bass("TRN2", target_bir_lowering=False)
        self.nc = nc
        self.xin = nc.dram_tensor("xin", [128, NC8 * S_LEN], F32, kind="ExternalInput").ap()
        self.vin = nc.dram_tensor("vin", [128, self.nvec], F32, kind="ExternalInput").ap()
        self.wpack = nc.dram_tensor("wpack", [128, self.wtotal], F32, kind="ExternalInput").ap()
        self.yout = nc.dram_tensor("yout", [128, NC8 * S_LEN], F32, kind="ExternalOutput").ap()
        with ExitStack() as es:
            def sb(name, cols, dt):
                return TT(name, es.enter_context(nc.sbuf_tensor(name, [128, cols], dt)))
            self.XT = sb("XT", NC8 * S_LEN, F32)
            self.XN = sb("XN", NC8 * S_LEN, BF16)
            self.SCR = sb("SCR", 11 * S_LEN, BF16)
            self.SLOT = [sb("SLOT%d" % i, SLOT_COLS, BF16) for i in range(NSLOT)]
            self.VEC = sb("VEC", self.nvec, F32)
            self.DER = sb("DER", 80, F32)
            self.ONES = sb("ONES", 128, BF16)
            self.TMP = sb("TMP", NTMP * TW, F32)
            self.SQB = sb("SQB", 4 * TS, BF16)
            self.UC = sb("UC", 3 * NH, F32)
            self.HC = sb("HC", NH, F32)
            self.PC = sb("PC", 15 * NG, F32)
            self.INVC = sb("INVC", 16, F32)
            self.P = [TT("PS%d" % i, es.enter_context(nc.psum_tensor("PS%d" % i, [128, TS], F32))) for i in range(8)]
            S = self.S
            S.dry = False
            S.reset()
            for e in Sched.ENGS:
                S.esem[e] = Sem(es.enter_context(nc.semaphore("sem_" + e)))
            self.slot_sem = [Sem(es.enter_context(nc.semaphore("sem_slot%d" % i))) for i in range(NSLOT)]
            self.s_in = Sem(es.enter_context(nc.semaphore("sem_in")))
            self.s_out = Sem(es.enter_context(nc.semaphore("sem_out")))
            self.W.reset_run()
            self.body()
            S.finalize(whole_sems=(self.s_in,))
            block = es.enter_context(nc.Block())
            s_out = self.s_out

            @block.tensor
            def _(e):
                S.emit_engine("pe", e)

            @block.scalar
            def _(e):
                S.emit_engine("act", e)

            @block.vector
            def _(e):
                S.emit_engine("dve", e)

            @block.gpsimd
            def _(e):
                S.emit_engine("pool", e)

            @block.sync
            def _(e):
                S.emit_engine("sp", e, tail=lambda eng: eng.wait_ge(s_out.h, s_out.total))
        return nc

    def _alloc_dummy(self):
        class _D:
            def __getitem__(self, k):
                return None
        dd = _D()
        mk = lambda n: TT(n, dd)
        self.XT, self.XN, self.SCR = mk("XT"), mk("XN"), mk("SCR")
        self.SLOT = [mk("SLOT%d" % i) for i in range(NSLOT)]
        self.VEC, self.DER, self.ONES, self.TMP, self.SQB = mk("VEC"), mk("DER"), mk("ONES"), mk("TMP"), mk("SQB")
        self.UC, self.HC, self.PC, self.INVC = mk("UC"), mk("HC"), mk("PC"), mk("INVC")
        self.P = [mk("PS%d" % i) for i in range(8)]
        self.xin = self.vin = self.yout = dd
        self.s_in = self.s_out = None
        self.slot_sem = [None] * NSLOT


_CACHE = {}


def _get_gen(nl):
    if nl not in _CACHE:
        g = Gen(nl)
        g.build()
        _CACHE[nl] = g
    return _CACHE[nl]


def kernel(_nl=4, **inputs):
    inp = {k: np.asarray(v) for k, v in inputs.items()}
    g = _get_gen(_nl)
    nc = g.nc
    wpack = np.empty((128, g.wtotal), np.float32)
    for (src, off, n) in g.src_list:
        wpack[:, off:off + n] = slab_source(src, inp)
    vecs = pack_vecs(inp, _nl)
    x = np.asarray(inp["x"], np.float32)
    B = x.shape[0]
    in_maps = []
    for b in range(B):
        xt = np.ascontiguousarray(x[b].T).reshape(NC8, 128, S_LEN).transpose(1, 0, 2).reshape(128, NC8 * S_LEN)
        in_maps.append({"xin": np.ascontiguousarray(xt), "vin": vecs, "wpack": wpack})
    res = run_bass_kernel_spmd(nc, in_maps, core_ids=list(range(B)))
    out = np.empty((B, S_LEN, D), np.float32)
    for b in range(B):
        y = np.asarray(res.results[b]["yout"]).reshape(128, NC8, S_LEN).transpose(1, 0, 2).reshape(D, S_LEN)
        out[b] = y.T
    return out
```

```python
import numpy as np
from contextlib import ExitStack
import concourse.bass as bass
import concourse.mybir as mybir
from concourse.bass_utils import run_bass_kernel_spmd

F32 = mybir.dt.float32
BF16 = mybir.dt.bfloat16
AF = mybir.ActivationFunctionType
ALU = mybir.AluOpType

D = 1024
S_LEN = 2048
NT = 4
TS = 512
NC8 = 8
DFF = 2816
NJ = 22
NH = 10
NG = 4
EPS = 1e-6
NSLOT = 5
SLOT_COLS = 3072
TW = 528
NTMP = 14
VPL = 112


class Sem:
    def __init__(self, h):
        self.h = h
        self.total = 0


class Ins:
    __slots__ = ("eng", "emit", "deps", "sig", "sem", "inc", "done", "isdma", "idx", "dur", "tset", "alld",
                 "succ", "nleft", "rt", "fin", "tag", "lab")

    def __init__(self, eng, emit, sem, inc, isdma):
        self.eng = eng
        self.emit = emit
        self.deps = []
        self.sig = isdma
        self.sem = sem
        self.inc = inc
        self.done = None
        self.isdma = isdma
        self.idx = 0
        self.dur = 500.0
        self.tset = None
        self.alld = []
        self.succ = []
        self.nleft = 0
        self.rt = 0.0
        self.fin = 0.0


class Sched:
    ENGS = ("pe", "act", "dve", "pool", "sp")

    def __init__(self):
        self.dry = True
        self.reset()

    def reset(self):
        self.all = []
        self.q = {e: [] for e in self.ENGS}
        self.recs = {}
        self.esem = {}

    def add(self, eng, emit, reads=(), writes=(), sem=None, inc=1, isdma=False, dur=500.0, tset=None, lab=""):
        if self.dry:
            return None
        if sem is None:
            sem = self.esem[eng]
        I = Ins(eng, emit, sem, inc, isdma)
        I.idx = len(self.all)
        I.tag = getattr(self, "tag", "")
        I.lab = lab
        I.dur = dur
        I.tset = tset
        deps = {}
        recs = self.recs
        for (tid, lo, hi) in reads:
            for r in recs.get(tid, ()):
                if r[0] < hi and lo < r[1] and r[2] is not None:
                    deps[id(r[2])] = (r[2], "RAW")
        for (tid, lo, hi) in writes:
            for r in recs.get(tid, ()):
                if r[0] < hi and lo < r[1]:
                    if r[2] is not None and id(r[2]) not in deps:
                        deps[id(r[2])] = (r[2], "WAW")
                    for rd in r[3]:
                        if id(rd) not in deps:
                            deps[id(rd)] = (rd, "WAR")
        for (tid, lo, hi) in reads:
            lst = recs.setdefault(tid, [])
            for r in lst:
                if r[0] == lo and r[1] == hi:
                    r[3].append(I)
                    break
            else:
                lst.append([lo, hi, None, [I]])
        for (tid, lo, hi) in writes:
            lst = recs.setdefault(tid, [])
            for r in lst:
                if r[0] == lo and r[1] == hi:
                    r[2] = I
                    r[3] = []
                    break
            else:
                lst.append([lo, hi, I, []])
        for (Dp, kind) in deps.values():
            if Dp is I:
                continue
            I.alld.append(Dp)
            if Dp.eng == eng and not Dp.isdma:
                if eng == "pe":
                    continue
                if kind != "RAW":
                    continue
            I.deps.append(Dp)
            Dp.sig = True
        self.all.append(I)
        self.q[eng].append(I)
        return I

    def schedule(self):
        HOP = 900.0
        SWITCH = 2700.0
        PATIENCE = 2500.0
        STICK = 100
        for I in self.all:
            I.nleft = len(I.alld)
            I.rt = 0.0
            for Dp in I.alld:
                Dp.succ.append(I)
        ready = {e: [] for e in self.ENGS}
        for I in self.all:
            if I.nleft == 0:
                ready[I.eng].append(I)
        free = {e: 0.0 for e in self.ENGS}
        order = {e: [] for e in self.ENGS}
        cur_set = None
        left = len(self.all)
        while left:
            best_e, best_t = None, None
            for e in self.ENGS:
                r = ready[e]
                if not r:
                    continue
                t = max(free[e], min(x.rt for x in r))
                if best_t is None or t < best_t:
                    best_e, best_t = e, t
            e, t = best_e, best_t
            r = ready[e]
            cands = [x for x in r if x.rt <= t]
            I = None
            if e == "act":
                oldest = min(x.idx for x in cands)
                same = [x for x in r if (x.tset is None or cur_set in x.tset) and x.rt <= t + PATIENCE]
                if same:
                    so = min(same, key=lambda x: (max(x.rt, t), x.idx))
                    if so.idx - oldest < STICK:
                        I = so
                        t = max(t, so.rt)
            if I is None:
                I = min(cands, key=lambda x: x.idx)
            r.remove(I)
            start = t
            if e == "act" and I.tset is not None and cur_set not in I.tset:
                start += SWITCH
                cur_set = sorted(I.tset)[0]
            if I.isdma:
                free[e] = start + 900.0
                I.fin = start + I.dur
            else:
                I.fin = start + I.dur
                free[e] = I.fin
            order[e].append(I)
            left -= 1
            for sc in I.succ:
                lat = I.fin + (HOP if (sc.eng != e or I.isdma) else 0.0)
                if lat > sc.rt:
                    sc.rt = lat
                sc.nleft -= 1
                if sc.nleft == 0:
                    ready[sc.eng].append(sc)
        self.q = order
        self.sim_end = max(free.values())

    def finalize(self, whole_sems=(), reorder=True):
        if reorder:
            self.schedule()
        for e in self.ENGS:
            for I in self.q[e]:
                if I.sig:
                    I.sem.total += I.inc
                    I.done = I.sem.total
        for I in self.all:
            if I.sig and any(I.sem is w for w in whole_sems):
                I.done = I.sem.total

    def emit_engine(self, engname, eng, tail=None):
        waited = {}
        for I in self.q[engname]:
            need = {}
            for Dp in I.deps:
                k = id(Dp.sem)
                if need.get(k, (None, 0))[1] < Dp.done:
                    need[k] = (Dp.sem, Dp.done)
            for k, (sem, val) in need.items():
                if waited.get(k, 0) < val:
                    eng.wait_ge(sem.h, val)
                    waited[k] = val
            r = I.emit(eng)
            if I.sig:
                if isinstance(r, list):
                    for x in r:
                        x.then_inc(I.sem.h, 16)
                else:
                    r.then_inc(I.sem.h, I.inc)
        if tail is not None:
            tail(eng)


class TT:
    def __init__(self, name, h):
        self.name = name
        self.h = h

    def v(self, lo, hi, rlo=None, rhi=None):
        return (self.h[:, lo:hi], (self.name, lo if rlo is None else rlo, hi if rhi is None else rhi))


class WStream:
    def __init__(self, S):
        self.S = S
        self.uses = []
        self.order = []
        self.last_pos = {}
        self.src_of = {}
        self.reset_run()

    def reset_run(self):
        self.pos = 0
        self.nloaded = 0
        self.slot_of = {}

    def finish_dry(self):
        seen = set()
        for i, k in enumerate(self.uses):
            if k not in seen:
                seen.add(k)
                self.order.append(k)
            self.last_pos[k] = i

    def get(self, key, src, n):
        if self.S.dry:
            self.uses.append(key)
            self.src_of[key] = (src, n)
            return 0
        assert self.uses[self.pos] == key
        self.pos += 1
        assert key in self.slot_of, ("slab not resident", key)
        idx = self.order.index(key) if False else self.slot_of[key]
        return idx

    def tick(self, loader):
        if self.S.dry:
            return
        while self.nloaded < len(self.order):
            n = self.nloaded
            if n >= NSLOT:
                old = self.order[n - NSLOT]
                if self.last_pos[old] >= self.pos:
                    break
                self.slot_of.pop(old, None)
            key = self.order[n]
            slot = n % NSLOT
            loader(key, slot)
            self.slot_of[key] = slot
            self.nloaded += 1


def _rows_slab(W, r0, nk, c0, ncols):
    blk = W[r0:r0 + nk * 128, c0:c0 + ncols].reshape(nk, 128, ncols)
    return np.ascontiguousarray(blk.transpose(1, 0, 2)).reshape(128, nk * ncols)


def slab_source(src, inp):
    kind = src[0]
    l = src[1]
    if kind == "up":
        W = inp["ffn1_w_up" if src[2] == 0 else "ffn2_w_up"][l]
        return _rows_slab(W, 0, 8, src[3], src[4])
    if kind == "wd":
        W = inp["ffn1_w_down" if src[2] == 0 else "ffn2_w_down"][l]
        return _rows_slab(W, src[3] * 11 * 128, 11, src[4], src[5])
    if kind == "win":
        return _rows_slab(inp["w_in"][l], 0, 8, src[2], src[3])
    if kind == "small":
        pw = inp["pool_w"][l]
        wa = inp["lru_w_a"][l]
        wx = inp["lru_w_x"][l]
        allw = np.concatenate([pw, wa, wx], axis=0)
        return np.ascontiguousarray(allw.transpose(1, 0, 2)).reshape(128, 24 * 128)
    if kind == "head":
        h = src[2]
        W = inp["w_in"][l]
        blks = [W[k * 128:(k + 1) * 128, 512 + 128 * h:512 + 128 * (h + 1)] for k in range(8)]
        blks += [W[k * 128:(k + 1) * 128, 1792 + 128 * h:1792 + 128 * (h + 1)] for k in range(8)]
        blks += [inp["lru_w_a"][l][h], inp["lru_w_x"][l][h]]
        return np.ascontiguousarray(np.stack(blks, axis=1)).reshape(128, 18 * 128)
    if kind == "pool1":
        W = inp["w_in"][l]
        blks = [W[k * 128:(k + 1) * 128, 384:512] for k in range(8)] + [inp["pool_w"][l][g] for g in range(4)]
        return np.ascontiguousarray(np.stack(blks, axis=1)).reshape(128, 12 * 128)
    if kind == "m3":
        c = src[2]
        W = inp["w_in"][l]
        blks = [W[k * 128:(k + 1) * 128, 3072 + 128 * c:3072 + 128 * (c + 1)] for k in range(8)]
        blks += [W[k * 128:(k + 1) * 128, 4096 + 128 * c:4096 + 128 * (c + 1)] for k in range(8)]
        PU = inp["w_pool_up"][l]
        blks += [PU[g * 128:(g + 1) * 128, 128 * c:128 * (c + 1)] for g in range(4)]
        return np.ascontiguousarray(np.stack(blks, axis=1)).reshape(128, 20 * 128)
    if kind == "pu":
        return _rows_slab(inp["w_pool_up"][l], 0, 4, src[2], src[3])
    if kind == "lu":
        return _rows_slab(inp["w_lru_up"][l], 0, 10, src[2], src[3])
    if kind == "wo":
        return _rows_slab(inp["w_out"][l], 0, 8, src[2], src[3])
    raise KeyError(src)


def pack_vecs(inp, nl):
    cols = []

    def fm(v, nchunk):
        return np.asarray(v, np.float32).reshape(nchunk, 128).T

    for l in range(nl):
        cols.append(fm(inp["norm_ffn1"][l], 8))
        cols.append(fm(inp["norm_mix"][l], 8))
        cols.append(fm(inp["norm_ffn2"][l], 8))
        cols.append(fm(inp["pool_b"][l].reshape(-1), 4))
        cols.append(fm(inp["pool_scale"][l], 4))
        cw = np.asarray(inp["conv_w"][l], np.float32)
        cwf = cw.reshape(4, NH, 128).transpose(2, 1, 0).reshape(128, NH * 4)
        cols.append(cwf)
        cols.append(fm(inp["conv_b"][l], NH))
        cols.append(fm(inp["lru_b_a"][l].reshape(-1), NH))
        cols.append(fm(inp["lru_b_x"][l].reshape(-1), NH))
        cols.append(fm(inp["lru_lambda"][l], NH))
    cols.append(fm(inp["final_norm"], 8))
    return np.ascontiguousarray(np.concatenate(cols, axis=1), dtype=np.float32)


class Gen:
    def __init__(self, nl):
        self.nl = nl
        self.S = Sched()
        self.W = WStream(self.S)
        self.pbi = 0

    def pb(self):
        b = self.P[self.pbi % 8]
        self.pbi += 1
        return b.v(0, TS)

    def xt(self, c, t):
        return self.XT.v(c * S_LEN + t * TS, c * S_LEN + (t + 1) * TS)

    def xn(self, c, t):
        return self.XN.v(c * S_LEN + t * TS, c * S_LEN + (t + 1) * TS)

    def tmp(self, i, lo=0, hi=TS):
        return self.TMP.v(i * TW + lo, i * TW + hi, i * TW, (i + 1) * TW)

    def scrf(self, j):
        lo = 14336 + j * 1024
        ap = self.SCR.h[:, lo:lo + 1024]
        return (ap.bitcast(F32) if ap is not None else None, ("SCR", lo, lo + 1024))

    def sqb(self, i):
        return self.SQB.v(i * TS, (i + 1) * TS)

    def vec(self, col, n=1):
        return self.VEC.v(col, col + n, 0, self.nvec)

    def der(self, col, n=1):
        return self.DER.v(col, col + n)

    @staticmethod
    def _sb(x, reads):
        if isinstance(x, tuple):
            reads.append(x[1])
            return x[0]
        return x

    def act(self, out, in_, func, scale=1.0, bias=0.0):
        reads = [in_[1]]
        sc = self._sb(scale, reads)
        bi = self._sb(bias, reads)
        o, i = out[0], in_[0]
        n = 0 if o is None else o.shape[-1]
        ts_ = {AF.Silu: frozenset("S"), AF.Gelu_apprx_tanh: frozenset("G"), AF.Tanh: frozenset("GS"),
               AF.Exp: frozenset("L"), AF.Ln: frozenset("L")}.get(func)
        self.S.add("act", lambda e: e.activation(out=o, in_=i, func=func, scale=sc, bias=bi), reads, [out[1]],
                   dur=60 + (224 + n) / 1.2, tset=ts_, lab=str(func).split('.')[-1])

    def ts(self, out, in0, s1, s2, op0, op1=None, eng="dve"):
        reads = [in0[1]]
        a1 = self._sb(s1, reads)
        a2 = self._sb(s2, reads) if s2 is not None else None
        o, i = out[0], in0[0]
        n = 0 if o is None else o.shape[-1]
        d_ = 135 + (n + 151) / 0.96
        if op1 is None:
            self.S.add(eng, lambda e: e.tensor_scalar(out=o, in0=i, scalar1=a1, scalar2=None, op0=op0), reads, [out[1]], dur=d_)
        else:
            self.S.add(eng, lambda e: e.tensor_scalar(out=o, in0=i, scalar1=a1, scalar2=a2, op0=op0, op1=op1), reads, [out[1]], dur=d_)

    def stt(self, out, in0, sc, in1, op0, op1):
        reads = [in0[1], in1[1]]
        a = self._sb(sc, reads)
        o, i0, i1 = out[0], in0[0], in1[0]
        n = 0 if o is None else o.shape[-1]
        self.S.add("dve", lambda e: e.scalar_tensor_tensor(out=o, in0=i0, scalar=a, in1=i1, op0=op0, op1=op1), reads, [out[1]],
                   dur=135 + (n + 151) / 0.96)

    def tt(self, out, in0, in1, op):
        o, i0, i1 = out[0], in0[0], in1[0]
        n = 0 if o is None else o.shape[-1]
        self.S.add("dve", lambda e: e.tensor_tensor(out=o, in0=i0, in1=i1, op=op), [in0[1], in1[1]], [out[1]],
                   dur=135 + (n + 151) / 0.96)

    def ptt(self, out, in0, in1, op):
        o, i0, i1 = out[0], in0[0], in1[0]
        n = 0 if o is None else o.shape[-1]
        self.S.add("pool", lambda e: e.tensor_tensor(out=o, in0=i0, in1=i1, op=op), [in0[1], in1[1]], [out[1]],
                   dur=100 + 2.3 * n)

    def pcopy(self, out, in_):
        o, i_ = out[0], in_[0]
        n = 0 if o is None else o.shape[-1]
        self.S.add("pool", lambda e: e.tensor_copy(out=o, in_=i_), [in_[1]], [out[1]], dur=200 + 4.3 * n)

    def memset(self, out, val):
        o = out[0]
        self.S.add("dve", lambda e: e.memset(o, val), [], [out[1]], dur=200.0)

    def mm(self, bank, pairs, first=True, last=True):
        reads = []
        aps = []
        for (l, r) in pairs:
            reads.append(l[1])
            reads.append(r[1])
            aps.append((l[0], r[0]))
        ob = bank[0]
        n = len(aps)

        def emit(e):
            ins = None
            for i, (la, ra) in enumerate(aps):
                ins = e.matmul(ob, lhsT=la, rhs=ra, start=(first and i == 0), stop=(last and i == n - 1))
            return ins
        self.S.add("pe", emit, reads, [bank[1]], dur=256.0 * n)

    def wget(self, key, src, nk, ncols):
        slot = self.W.get(key, src, nk * ncols)
        return (slot, nk, ncols)

    def wl(self, slab, k, j):
        slot, nk, ncols = slab
        lo = k * ncols + j * 128
        return self.SLOT[slot].v(lo, lo + 128, 0, SLOT_COLS)

    def wtick(self):
        self.W.tick(self._load)

    def _load(self, key, slot):
        src, n = self.W.src_of[key]
        off = self.src_off[src]
        o = self.SLOT[slot].h[:, 0:n]
        i = self.wpack[:, off:off + n]
        sem = self.slot_sem[slot]
        self.S.add("pool", lambda e: [e.dma_start(out=o, in_=i)], [], [(self.SLOT[slot].name, 0, SLOT_COLS)],
                   sem=sem, inc=16, isdma=True, dur=3000.0 + n * 512 / 180.0)

    def norm(self, gcol, tiles, final=False):
        for t in tiles:
            bank = self.pb()
            for c in range(NC8):
                sq = self.sqb((t * NC8 + c) % 4)
                if c % 2 == 0:
                    self.act(sq, self.xt(c, t), AF.Square)
                else:
                    self.tt(sq, self.xt(c, t), self.xt(c, t), ALU.mult)
                self.mm(bank, [(self.ONES.v(0, 128), sq)], first=(c == 0), last=(c == NC8 - 1))
            r = self.tmp(3 + (t % 4))
            self.act(r, bank, AF.Ln, scale=1.0 / D, bias=self.der(31))
            self.act(r, r, AF.Exp, scale=-0.5)
            for c in range(NC8):
                out = self.xt(c, t) if final else self.xn(c, t)
                self.stt(out, self.xt(c, t), self.vec(gcol + c), r, ALU.mult, ALU.mult)

    def ffn(self, l, which):
        vb = l * VPL
        self.S.tag = "ffn%d.%d" % (which, l)
        self.norm(vb + (0 if which == 0 else 16), range(NT))
        groups = [(0, 3), (3, 3), (6, 3), (9, 2)]
        si = 0
        for half in range(2):
            for gi, (j0, nj) in enumerate(groups):
                ca = (half * 11 + j0) * 128
                wa = self.wget(("ua", l, which, half, gi), ("up", l, which, ca, nj * 128), 8, nj * 128)
                wb = self.wget(("ub", l, which, half, gi), ("up", l, which, DFF + ca, nj * 128), 8, nj * 128)
                for jl in range(nj):
                    jj = j0 + jl
                    for t in range(NT):
                        A = self.pb()
                        B = self.pb()
                        self.mm(A, [(self.wl(wa, k, jl), self.xn(k, t)) for k in range(NC8)])
                        self.mm(B, [(self.wl(wb, k, jl), self.xn(k, t)) for k in range(NC8)])
                        s = self.tmp(si % 3)
                        si += 1
                        self.act(s, A, AF.Silu)
                        hv = self.SCR.v(jj * S_LEN + t * TS, jj * S_LEN + (t + 1) * TS)
                        self.tt(hv, B, s, ALU.mult)
                self.wtick()
            tpasses = [range(NT)] if half == 0 else [range(0, 2), range(2, 4)]
            for pi, trange in enumerate(tpasses):
                for cg in range(4):
                    wd = self.wget(("wd", l, which, half, cg, pi), ("wd", l, which, half, cg * 256, 256), 11, 256)
                    for ci in range(2):
                        c = cg * 2 + ci
                        for t in trange:
                            Y = self.pb()
                            self.mm(Y, [(self.wl(wd, jj, ci), self.SCR.v(jj * S_LEN + t * TS, jj * S_LEN + (t + 1) * TS))
                                        for jj in range(11)])
                            self.stt(self.xt(c, t), Y, 0.5, self.xt(c, t), ALU.mult, ALU.add)
                    self.wtick()

    def derive(self, l):
        vb = l * VPL
        lam = self.vec(vb + 102, NH)
        d = lambda c: self.der(c, NH)
        c2, hba, hbx = d(0), d(10), d(20)
        t0, t1, t2, t3 = self.der(40, NH), self.der(50, NH), self.der(60, NH), self.der(70, NH)
        self.ts(t0, lam, -1.0, None, ALU.mult)
        self.tt(t1, lam, t0, ALU.max)
        self.act(t1, t1, AF.Exp, scale=-1.0)
        self.ts(t2, t1, 2.0, None, ALU.add)
        o2, i2 = t2[0], t2[0]
        self.S.add("dve", lambda e: e.reciprocal(out=o2, in_=i2), [t2[1]], [t2[1]], dur=400.0)
        self.tt(t2, t1, t2, ALU.mult)
        self.tt(t3, t2, t2, ALU.mult)
        self.ts(t1, t3, 1.0 / 11.0, 1.0 / 9.0, ALU.mult, ALU.add)
        for cst in (1.0 / 7.0, 1.0 / 5.0, 1.0 / 3.0, 1.0):
            self.tt(t1, t1, t3, ALU.mult)
            self.ts(t1, t1, cst, None, ALU.add)
        self.tt(t1, t1, t2, ALU.mult)
        self.ts(t0, t0, 0.0, None, ALU.max)
        self.stt(t1, t1, 2.0, t0, ALU.mult, ALU.add)
        self.ts(c2, t1, -4.0, None, ALU.mult)
        self.ts(hba, self.vec(vb + 82, NH), 0.5, None, ALU.mult)
        self.ts(hbx, self.vec(vb + 92, NH), 0.5, None, ALU.mult)

    def mixer(self, l):
        vb = l * VPL
        self.derive(l)
        self.memset(self.UC.v(0, 3 * NH), 0.0)
        self.memset(self.HC.v(0, NH), 0.0)
        self.memset(self.PC.v(0, 15 * NG), 0.0)
        PM0, HG0, M0 = 0, 4096, 14336
        for half in range(2):
            tiles = [2 * half, 2 * half + 1]
            self.S.tag = "mixnorm%d.%d" % (half, l)
            self.norm(vb + 8, tiles)
            self.S.tag = "m1.%d.%d" % (half, l)
            wp0 = self.wget(("winp0", l, half), ("win", l, 0, 384), 8, 384)
            wp1 = self.wget(("pool1", l, half), ("pool1", l), 12, 128)
            ubank = {}

            def m1a(g):
                slab, jl = (wp0, g) if g < 3 else (wp1, 0)
                pcv = self.PC.v(15 * g, 15 * g + 15)
                for ti, t in enumerate(tiles):
                    ui = 2 * (g % 2) + ti
                    U = self.pb()
                    self.mm(U, [(self.wl(slab, k, jl), self.xn(k, t)) for k in range(NC8)])
                    if ti == 0:
                        self.act(self.tmp(ui, 0, 15), pcv, AF.Copy)
                    else:
                        self.act(self.tmp(ui, 0, 15), self.tmp(ui - 1, 512, 527), AF.Copy)
                    self.act(self.tmp(ui, 15, 527), U, AF.Copy)
                    if ti == len(tiles) - 1:
                        self.act(pcv, self.tmp(ui, 512, 527), AF.Copy)

            def m1b(g):
                w = 2 << g
                for ti, t in enumerate(tiles):
                    ui = 2 * (g % 2) + ti
                    cur, ln, d, flip = ui, 527, 1, 0
                    while d < w:
                        nxt = 4 + ((ti * 2 + flip) % 4)
                        flip ^= 1
                        self.tt(self.tmp(nxt, 0, ln - d), self.tmp(cur, d, ln), self.tmp(cur, 0, ln - d), ALU.add)
                        cur, ln, d = nxt, ln - d, d * 2
                    o0 = 16 - w
                    pl = self.sqb(ti)
                    self.stt(pl, self.tmp(cur, o0, o0 + TS), 1.0 / w, self.tmp(ui, 15, 527), ALU.mult, ALU.subtract)
                    if t == 0:
                        nfix = w - 1
                        fx = self.tmp(8, 0, nfix)
                        self.tt(fx, self.tmp(cur, o0, o0 + nfix), self.INVC.v(0, nfix), ALU.mult)
                        self.tt(self.SQB.v(ti * TS, ti * TS + nfix, ti * TS, (ti + 1) * TS), fx,
                                self.tmp(ui, 15, 15 + nfix), ALU.subtract)
                    PMb = self.pb()
                    self.mm(PMb, [(self.wl(wp1, 8 + g, 0), pl)])
                    pmv = self.SCR.v(PM0 + g * 1024 + ti * TS, PM0 + g * 1024 + (ti + 1) * TS)
                    self.ts(pmv, PMb, self.vec(vb + 24 + g), self.vec(vb + 28 + g), ALU.add, ALU.mult)

            m1a(0)
            for g in range(NG):
                if g + 1 < NG:
                    m1a(g + 1)
                m1b(g)
            self.wtick()
            self.S.tag = "m2.%d.%d" % (half, l)
            nti = len(tiles)
            banks = {}

            xoff = 1024 if half == 0 else 0

            def xnf(j):
                lo = j * S_LEN + xoff
                ap = self.XN.h[:, lo:lo + 1024]
                return (ap.bitcast(F32) if ap is not None else None, ("XN", lo, lo + 1024))

            def sub(view, lo, hi):
                return (view[0][:, lo:hi] if view[0] is not None else None, view[1])

            def B_ut(h, ti, lo=0, hi=TS):
                return self.tmp(2 * (h % 2) + ti, lo, hi)

            def B_v(h, ti):
                return self.tmp(4 + 2 * (h % 2) + ti)

            def B_tra(h, ti):
                return self.tmp(8 + 2 * (h % 2) + ti)

            def B_tiq(h, ti):
                return self.scrf(2 * (h % 2) + ti)

            def B_a2m(h, ti):
                return self.scrf(4 + 2 * (h % 2) + ti)

            def B_hh(h, ti):
                return xnf(2 * (h % 2) + ti)

            def B_gl(h, ti):
                return xnf(4 + 2 * (h % 2) + ti)

            def phi1a(h, wu, wg, jl):
                ucv = self.UC.v(3 * h, 3 * h + 3)
                for ti, t in enumerate(tiles):
                    Ub = self.pb()
                    Gb = self.pb()
                    self.mm(Ub, [(self.wl(wu, k, 0), self.xn(k, t)) for k in range(NC8)])
                    self.mm(Gb, [(self.wl(wu, 8 + k, 0), self.xn(k, t)) for k in range(NC8)])
                    if ti == 0:
                        self.pcopy(B_ut(h, ti, 0, 3), ucv)
                    else:
                        self.pcopy(B_ut(h, ti, 0, 3), B_ut(h, ti - 1, 512, 515))
                    self.act(B_ut(h, ti, 3, 515), Ub, AF.Copy)
                    if ti == nti - 1:
                        self.pcopy(ucv, B_ut(h, ti, 512, 515))
                    self.act(B_gl(h, ti), Gb, AF.Gelu_apprx_tanh)
                for ti, t in enumerate(tiles):
                    v = B_v(h, ti)
                    cw = vb + 32 + h * 4
                    self.ts(v, B_ut(h, ti, 0, 512), self.vec(cw), self.vec(vb + 72 + h), ALU.mult, ALU.add)
                    for k in range(1, 4):
                        self.stt(v, B_ut(h, ti, k, k + 512), self.vec(cw + k), v, ALU.mult, ALU.add)

            def phi1b(h, wsm):
                for ti, t in enumerate(tiles):
                    vbf = self.sqb(2 * (h % 2) + ti)
                    self.act(vbf, B_v(h, ti), AF.Copy)
                    Rb = self.pb()
                    Ib = self.pb()
                    self.mm(Rb, [(self.wl(wsm, 16, 0), vbf)])
                    self.mm(Ib, [(self.wl(wsm, 17, 0), vbf)])
                    banks[(h, ti)] = (Rb, Ib)

            def a3t(h):
                for ti in range(nti):
                    Rb, Ib = banks[(h, ti)]
                    self.act(B_tra(h, ti), Rb, AF.Tanh, scale=0.5, bias=self.der(10 + h))
                    self.act(B_tiq(h, ti), Ib, AF.Tanh, scale=0.5, bias=self.der(20 + h))
                    self.stt(B_tiq(h, ti), B_tiq(h, ti), 1.0, B_v(h, ti), ALU.add, ALU.mult)

            def a3l(hs):
                for h in hs:
                    for ti in range(nti):
                        tra, a2m = B_tra(h, ti), B_a2m(h, ti)
                        self.act(tra, tra, AF.Exp, scale=self.der(h), bias=self.der(h))
                        self.act(a2m, tra, AF.Square)
                for h in hs:
                    for ti in range(nti):
                        a2m = B_a2m(h, ti)
                        self.act(a2m, a2m, AF.Ln, scale=-1.0, bias=self.der(30))
                for h in hs:
                    for ti in range(nti):
                        a2m = B_a2m(h, ti)
                        self.act(a2m, a2m, AF.Exp, scale=0.5, bias=self.der(32))

            def d2(h):
                hcv = self.HC.v(h, h + 1)
                for ti in range(nti):
                    tra, tiq, a2m, hh = B_tra(h, ti), B_tiq(h, ti), B_a2m(h, ti), B_hh(h, ti)
                    self.tt(tiq, tiq, a2m, ALU.mult)
                    iv = hcv if ti == 0 else sub(B_hh(h, ti - 1), 511, 512)
                    o, d0, d1, ini = hh[0], tra[0], tiq[0], iv[0]
                    self.S.add("dve", lambda e, o=o, d0=d0, d1=d1, ini=ini: e.tensor_tensor_scan(
                        out=o, data0=d0, data1=d1, initial=ini, op0=ALU.mult, op1=ALU.add),
                        [tra[1], tiq[1], iv[1]], [hh[1]], dur=1420.0)
                    if ti == nti - 1:
                        self.pcopy(hcv, sub(hh, 511, 512))
                    hgv = self.SCR.v(HG0 + h * 1024 + ti * TS, HG0 + h * 1024 + (ti + 1) * TS)
                    self.ptt(hgv, hh, B_gl(h, ti), ALU.mult)

            hslab = {}
            for i in range(NH + 1):
                if i < NH:
                    hslab[i] = self.wget(("head", l, half, i), ("head", l, i), 18, 128)
                    phi1a(i, hslab[i], None, 0)
                if i >= 1:
                    a3t(i - 1)
                    a3l([i - 1])
                if i < NH:
                    phi1b(i, hslab[i])
                    self.wtick()
                if i >= 1:
                    d2(i - 1)
            self.S.tag = "m3.%d.%d" % (half, l)
            for c in range(NC8):
                m3s = self.wget(("m3", l, half, c), ("m3", l, c), 20, 128)
                lu = self.wget(("lu", l, half, c // 2), ("lu", l, 256 * (c // 2), 256), NH, 256)
                for ti, t in enumerate(tiles):
                    GP, GL, YP, YL = self.pb(), self.pb(), self.pb(), self.pb()
                    self.mm(GP, [(self.wl(m3s, k, 0), self.xn(k, t)) for k in range(NC8)])
                    self.mm(GL, [(self.wl(m3s, 8 + k, 0), self.xn(k, t)) for k in range(NC8)])
                    self.mm(YP, [(self.wl(m3s, 16 + g, 0), self.SCR.v(PM0 + g * 1024 + ti * TS, PM0 + g * 1024 + (ti + 1) * TS))
                                 for g in range(NG)])
                    self.mm(YL, [(self.wl(lu, h, c % 2), self.SCR.v(HG0 + h * 1024 + ti * TS, HG0 + h * 1024 + (ti + 1) * TS))
                                 for h in range(NH)])
                    tgp, tgl = self.tmp(0 + ti), self.tmp(2 + ti)
                    self.act(tgp, GP, AF.Tanh, scale=0.5)
                    self.act(tgl, GL, AF.Tanh, scale=0.5)
                    self.stt(tgp, tgp, 1.0, YP, ALU.add, ALU.mult)
                    self.stt(tgl, tgl, 1.0, YL, ALU.add, ALU.mult)
                    mv = self.SCR.v(M0 + c * 1024 + ti * TS, M0 + c * 1024 + (ti + 1) * TS)
                    self.tt(mv, tgp, tgl, ALU.add)
                self.wtick()
            self.S.tag = "m4.%d.%d" % (half, l)
            for c in range(NC8):
                c3 = c // 3
                n3 = 3 if c3 < 2 else 2
                wo = self.wget(("wo", l, half, c3), ("wo", l, 384 * c3, n3 * 128), 8, n3 * 128)
                for ti, t in enumerate(tiles):
                    O = self.pb()
                    self.mm(O, [(self.wl(wo, k, c % 3), self.SCR.v(M0 + k * 1024 + ti * TS, M0 + k * 1024 + (ti + 1) * TS))
                                for k in range(NC8)])
                    self.stt(self.xt(c, t), O, 0.5, self.xt(c, t), ALU.mult, ALU.add)
                self.wtick()

    def body(self):
        self.pbi = 0
        S = self.S
        self.memset(self.ONES.v(0, 128), 1.0)
        for j in range(15):
            self.memset(self.INVC.v(j, j + 1), 1.0 / (j + 1))
        self.memset(self.der(30), 1.0)
        self.memset(self.der(31), EPS)
        self.memset(self.der(32), float(np.log(0.5)))
        xin, vin = self.xin, self.vin
        o = self.VEC.h[:, 0:self.nvec]
        S.add("sp", lambda e, o=o: [e.dma_start(out=o, in_=vin[:, :])], [], [("VEC", 0, self.nvec)],
              sem=self.s_in, inc=16, isdma=True, dur=4000.0)
        for c in range(NC8):
            o = self.XT.h[:, c * S_LEN:(c + 1) * S_LEN]
            i = xin[:, c * S_LEN:(c + 1) * S_LEN]
            S.add("sp", lambda e, o=o, i=i: [e.dma_start(out=o, in_=i)], [], [("XT", c * S_LEN, (c + 1) * S_LEN)],
                  sem=self.s_in, inc=16, isdma=True, dur=9000.0)
        self.wtick()
        for l in range(self.nl):
            self.ffn(l, 0)
            self.mixer(l)
            self.ffn(l, 1)
        self.norm(self.nl * VPL, range(NT), final=True)
        yout = self.yout
        for c in range(NC8):
            i = self.XT.h[:, c * S_LEN:(c + 1) * S_LEN]
            o = yout[:, c * S_LEN:(c + 1) * S_LEN]
            S.add("sp", lambda e, o=o, i=i: [e.dma_start(out=o, in_=i)], [("XT", c * S_LEN, (c + 1) * S_LEN)], [],
                  sem=self.s_out, inc=16, isdma=True, dur=9000.0)

    def build(self):
        nl = self.nl
        self.nvec = nl * VPL + 8
        self.S.dry = True
        self._alloc_dummy()
        self.body()
        self.W.finish_dry()
        self.src_off = {}
        off = 0
        self.src_list = []
        for key in self.W.order:
            src, n = self.W.src_of[key]
            if src not in self.src_off:
                self.src_off[src] = off
                self.src_list.append((src, off, n))
                off += n
        self.wtotal = off
        nc = bass.Bass("TRN2", target_bir_lowering=False)
        self.nc = nc
        self.xin = nc.dram_tensor("xin", [128, NC8 * S_LEN], F32, kind="ExternalInput").ap()
        self.vin = nc.dram_tensor("vin", [128, self.nvec], F32, kind="ExternalInput").ap()
        self.wpack = nc.dram_tensor("wpack", [128, self.wtotal], F32, kind="ExternalInput").ap()
        self.yout = nc.dram_tensor("yout", [128, NC8 * S_LEN], F32, kind="ExternalOutput").ap()
        with ExitStack() as es:
            def sb(name, cols, dt):
                return TT(name, es.enter_context(nc.sbuf_tensor(name, [128, cols], dt)))
            self.XT = sb("XT", NC8 * S_LEN, F32)
            self.XN = sb("XN", NC8 * S_LEN, BF16)
            self.SCR = sb("SCR", 11 * S_LEN, BF16)
            self.SLOT = [sb("SLOT%d" % i, SLOT_COLS, BF16) for i in range(NSLOT)]
            self.VEC = sb("VEC", self.nvec, F32)
            self.DER = sb("DER", 80, F32)
            self.ONES = sb("ONES", 128, BF16)
            self.TMP = sb("TMP", NTMP * TW, F32)
            self.SQB = sb("SQB", 4 * TS, BF16)
            self.UC = sb("UC", 3 * NH, F32)
            self.HC = sb("HC", NH, F32)
            self.PC = sb("PC", 15 * NG, F32)
            self.INVC = sb("INVC", 16, F32)
            self.P = [TT("PS%d" % i, es.enter_context(nc.psum_tensor("PS%d" % i, [128, TS], F32))) for i in range(8)]
            S = self.S
            S.dry = False
            S.reset()
            for e in Sched.ENGS:
                S.esem[e] = Sem(es.enter_context(nc.semaphore("sem_" + e)))
            self.slot_sem = [Sem(es.enter_context(nc.semaphore("sem_slot%d" % i))) for i in range(NSLOT)]
            self.s_in = Sem(es.enter_context(nc.semaphore("sem_in")))
            self.s_out = Sem(es.enter_context(nc.semaphore("sem_out")))
            self.W.reset_run()
            self.body()
            S.finalize(whole_sems=(self.s_in,))
            block = es.enter_context(nc.Block())
            s_out = self.s_out

            @block.tensor
            def _(e):
                S.emit_engine("pe", e)

            @block.scalar
            def _(e):
                S.emit_engine("act", e)

            @block.vector
            def _(e):
                S.emit_engine("dve", e)

            @block.gpsimd
            def _(e):
                S.emit_engine("pool", e)

            @block.sync
            def _(e):
                S.emit_engine("sp", e, tail=lambda eng: eng.wait_ge(s_out.h, s_out.total))
        return nc

    def _alloc_dummy(self):
        class _D:
            def __getitem__(self, k):
                return None
        dd = _D()
        mk = lambda n: TT(n, dd)
        self.XT, self.XN, self.SCR = mk("XT"), mk("XN"), mk("SCR")
        self.SLOT = [mk("SLOT%d" % i) for i in range(NSLOT)]
        self.VEC, self.DER, self.ONES, self.TMP, self.SQB = mk("VEC"), mk("DER"), mk("ONES"), mk("TMP"), mk("SQB")
        self.UC, self.HC, self.PC, self.INVC = mk("UC"), mk("HC"), mk("PC"), mk("INVC")
        self.P = [mk("PS%d" % i) for i in range(8)]
        self.xin = self.vin = self.yout = dd
        self.s_in = self.s_out = None
        self.slot_sem = [None] * NSLOT


_CACHE = {}


def _get_gen(nl):
    if nl not in _CACHE:
        g = Gen(nl)
        g.build()
        _CACHE[nl] = g
    return _CACHE[nl]


def kernel(_nl=4, **inputs):
    inp = {k: np.asarray(v) for k, v in inputs.items()}
    g = _get_gen(_nl)
    nc = g.nc
    wpack = np.empty((128, g.wtotal), np.float32)
    for (src, off, n) in g.src_list:
        wpack[:, off:off + n] = slab_source(src, inp)
    vecs = pack_vecs(inp, _nl)
    x = np.asarray(inp["x"], np.float32)
    B = x.shape[0]
    in_maps = []
    for b in range(B):
        xt = np.ascontiguousarray(x[b].T).reshape(NC8, 128, S_LEN).transpose(1, 0, 2).reshape(128, NC8 * S_LEN)
        in_maps.append({"xin": np.ascontiguousarray(xt), "vin": vecs, "wpack": wpack})
    res = run_bass_kernel_spmd(nc, in_maps, core_ids=list(range(B)))
    out = np.empty((B, S_LEN, D), np.float32)
    for b in range(B):
        y = np.asarray(res.results[b]["yout"]).reshape(128, NC8, S_LEN).transpose(1, 0, 2).reshape(D, S_LEN)
        out[b] = y.T
    return out
```
